# Optimizing a Trainium2 kernel written in Bass

```python
import math
import jax, jax.numpy as jnp
from jax import lax
import numpy as np

D_MODEL = 1024
BATCH = 4
SEQ = 4096
DEPTH = 2

N_A = DEPTH // 2
N_B = DEPTH - N_A
D_FF = 2816
LRU_WIDTH = D_MODEL
LRU_BLOCKS = 4
LRU_BW = LRU_WIDTH // LRU_BLOCKS
CONV_W = 4
LRU_C = 8.0
N_HEADS = 16
HEAD_DIM = 64
D_ATTN = N_HEADS * HEAD_DIM
Q_BLOCK = 128
EPS = 1e-6

kernel_name = "yoco_rglru_stickbreaking_macaron"


def rms_norm(x, gain):
    xf = x.astype(jnp.float32)
    y = xf * lax.rsqrt(jnp.mean(xf * xf, axis=-1, keepdims=True) + EPS)
    return (y * gain.astype(jnp.float32)).astype(x.dtype)


def swiglu(h, w13, w2):
    gate, up = jnp.split(h @ w13, 2, axis=-1)
    return (jax.nn.silu(gate) * up) @ w2


def causal_depthwise_conv(x, w, b):
    S = x.shape[1]
    xp = jnp.pad(x, ((0, 0), (CONV_W - 1, 0), (0, 0)))
    y = b
    for k in range(CONV_W):
        y = y + w[k] * xp[:, k:k + S]
    return y


def block_diag_linear(x, w, b):
    Bt, S, _ = x.shape
    xb = x.reshape(Bt, S, LRU_BLOCKS, LRU_BW)
    y = jnp.einsum('bsnc,ncd->bsnd', xb, w).reshape(Bt, S, LRU_WIDTH)
    return y + b


def linear_recurrence(a, u):
    def step(h, au):
        a_t, u_t = au
        h = a_t * h + u_t
        return h, h
    h0 = jnp.zeros((a.shape[0], a.shape[2]), jnp.float32)
    _, hs = lax.scan(step, h0, (jnp.swapaxes(a, 0, 1), jnp.swapaxes(u, 0, 1)))
    return jnp.swapaxes(hs, 0, 1)


def rglru_block(h, w_in, conv_w, conv_b, w_r, b_r, w_i, b_i, lam, w_out):
    gate_br, rec = jnp.split(h @ w_in, 2, axis=-1)
    gate_br = jax.nn.gelu(gate_br)
    xc = causal_depthwise_conv(rec, conv_w, conv_b)
    r = jax.nn.sigmoid(block_diag_linear(xc, w_r, b_r))
    i = jax.nn.sigmoid(block_diag_linear(xc, w_i, b_i))
    log_a = -LRU_C * r.astype(jnp.float32) * jax.nn.softplus(-lam.astype(jnp.float32))
    a = jnp.exp(log_a)
    mult = jnp.sqrt(-jnp.expm1(2.0 * log_a))
    u = mult * (i * xc).astype(jnp.float32)
    hs = linear_recurrence(a, u)
    y = hs.astype(h.dtype) * gate_br
    return y @ w_out


def heads(t):
    Bt, S, _ = t.shape
    return t.reshape(Bt, S, N_HEADS, HEAD_DIM).transpose(0, 2, 1, 3)


def shared_kv(x, kv_gain, w_kv, k_gain):
    hkv = rms_norm(x, kv_gain)
    k, v = jnp.split(hkv @ w_kv, 2, axis=-1)
    k = rms_norm(heads(k), k_gain)
    return k, heads(v)


def stick_breaking_attention(q, k, v):
    S = q.shape[2]
    scale = HEAD_DIM ** -0.5
    outs = []
    for blk in range(S // Q_BLOCK):
        q0 = blk * Q_BLOCK
        end = q0 + Q_BLOCK
        z = jnp.einsum('bhqd,bhkd->bhqk', q[:, :, q0:end], k[:, :, :end]).astype(jnp.float32) * scale
        t_pos = q0 + jnp.arange(Q_BLOCK)[:, None]
        s_pos = jnp.arange(end)[None, :]
        mask = s_pos < t_pos
        log_stay = jnp.where(mask, -jax.nn.softplus(z), 0.0)
        stay_after = lax.cumsum(log_stay, axis=3, reverse=True) - log_stay
        log_w = jax.nn.log_sigmoid(z) + stay_after
        w = jnp.where(mask, jnp.exp(log_w), 0.0)
        o = jnp.einsum('bhqk,bhkd->bhqd', w, v[:, :, :end].astype(jnp.float32))
        outs.append(o.astype(v.dtype))
    return jnp.concatenate(outs, axis=2)


def stick_breaking_block(h, k, v, w_q, q_gain, w_o):
    q = rms_norm(heads(h @ w_q), q_gain)
    o = stick_breaking_attention(q, k, v)
    Bt, _, S, _ = o.shape
    o = o.transpose(0, 2, 1, 3).reshape(Bt, S, D_ATTN)
    return o @ w_o


def setup_inputs(seed: int = 0) -> dict:
    key = jax.random.key(seed)
    keys = jax.random.split(key, 32)
    counter = [0]
    f32 = jnp.float32

    def nk():
        k = keys[counter[0]]
        counter[0] += 1
        return k

    def w(shape, fan_in, scale=1.0):
        return jax.random.normal(nk(), shape, f32) * (scale * fan_in ** -0.5)

    def gain(shape):
        return 1.0 + 0.02 * jax.random.normal(nk(), shape, f32)

    def bias(shape):
        return 0.02 * jax.random.normal(nk(), shape, f32)

    a0 = jax.random.uniform(nk(), (N_A, LRU_WIDTH), f32, 0.9, 0.999)
    return {
        "x": jax.random.normal(nk(), (BATCH, SEQ, D_MODEL), f32),
        "ffn1_norm": gain((DEPTH, D_MODEL)),
        "ffn1_w13": w((DEPTH, D_MODEL, 2 * D_FF), D_MODEL),
        "ffn1_w2": w((DEPTH, D_FF, D_MODEL), D_FF, 0.5),
        "mix_norm": gain((DEPTH, D_MODEL)),
        "a_w_in": w((N_A, D_MODEL, 2 * LRU_WIDTH), D_MODEL),
        "a_conv_w": w((N_A, CONV_W, LRU_WIDTH), CONV_W),
        "a_conv_b": bias((N_A, LRU_WIDTH)),
        "a_w_r": w((N_A, LRU_BLOCKS, LRU_BW, LRU_BW), LRU_BW),
        "a_b_r": bias((N_A, LRU_WIDTH)),
        "a_w_i": w((N_A, LRU_BLOCKS, LRU_BW, LRU_BW), LRU_BW),
        "a_b_i": bias((N_A, LRU_WIDTH)),
        "a_lambda": jnp.log(a0) - jnp.log1p(-a0),
        "a_w_out": w((N_A, LRU_WIDTH, D_MODEL), LRU_WIDTH),
        "kv_norm": gain((D_MODEL,)),
        "w_kv": w((D_MODEL, 2 * D_ATTN), D_MODEL),
        "k_norm": gain((HEAD_DIM,)),
        "b_w_q": w((N_B, D_MODEL, D_ATTN), D_MODEL),
        "q_norm": gain((N_B, HEAD_DIM)),
        "b_w_o": w((N_B, D_ATTN, D_MODEL), D_ATTN),
        "ffn2_norm": gain((DEPTH, D_MODEL)),
        "ffn2_w13": w((DEPTH, D_MODEL, 2 * D_FF), D_MODEL),
        "ffn2_w2": w((DEPTH, D_FF, D_MODEL), D_FF, 0.5),
    }


def reference(x, ffn1_norm, ffn1_w13, ffn1_w2, mix_norm, a_w_in, a_conv_w, a_conv_b,
              a_w_r, a_b_r, a_w_i, a_b_i, a_lambda, a_w_out, kv_norm, w_kv, k_norm,
              b_w_q, q_norm, b_w_o, ffn2_norm, ffn2_w13, ffn2_w2):
    k_shared = None
    v_shared = None
    for l in range(DEPTH):
        if l == N_A:
            k_shared, v_shared = shared_kv(x, kv_norm, w_kv, k_norm)
        x = x + 0.5 * swiglu(rms_norm(x, ffn1_norm[l]), ffn1_w13[l], ffn1_w2[l])
        h = rms_norm(x, mix_norm[l])
        if l < N_A:
            x = x + rglru_block(h, a_w_in[l], a_conv_w[l], a_conv_b[l], a_w_r[l], a_b_r[l],
                                a_w_i[l], a_b_i[l], a_lambda[l], a_w_out[l])
        else:
            j = l - N_A
            x = x + stick_breaking_block(h, k_shared, v_shared, b_w_q[j], q_norm[j], b_w_o[j])
        x = x + 0.5 * swiglu(rms_norm(x, ffn2_norm[l]), ffn2_w13[l], ffn2_w2[l])
    return x
```

```python
import numpy as np
from contextlib import ExitStack
import concourse.bass as bass
import concourse.mybir as mybir
from concourse.bass_utils import run_bass_kernel_spmd

F32 = mybir.dt.float32
BF16 = mybir.dt.bfloat16
AF = mybir.ActivationFunctionType
ALU = mybir.AluOpType

NT = 2048
TT = 512
NTT = NT // TT
DM = 1024
KC = 8
DFF = 2816
NU = 11
EPS = 1e-6
ROT = 4000
PAIRS = [[0, 1], [2, 3], [4, 5], [6, 7]]

V_F1N0, V_MIX0, V_F2N0, V_KVN, V_F1N1, V_MIX1, V_F2N1 = 0, 8, 16, 24, 32, 40, 48
V_CONVW, V_CONVB, V_BR, V_BI, V_LAM, V_KG, V_QG = 56, 88, 96, 104, 112, 120, 121
NV = 128


class Res:
    __slots__ = ("name", "w", "r")

    def __init__(self, name):
        self.name = name
        self.w = None
        self.r = []


class Ins:
    __slots__ = ("eng", "fn", "kind", "key", "deps", "need", "sig")

    def __init__(self, eng, fn, kind, key=None):
        self.eng, self.fn, self.kind, self.key = eng, fn, kind, key
        self.deps = []
        self.need = False
        self.sig = None


class Prog:
    ENGS = ("pe", "act", "dve", "pool", "sp")

    def __init__(self):
        self.streams = {k: [] for k in self.ENGS}
        self.dma_cnt = {}
        self.ncc = 0

    def _add(self, ins, reads, writes):
        deps = []
        seen = set()

        def add(d):
            if d is None or d is ins or id(d) in seen:
                return
            seen.add(id(d))
            deps.append(d)

        raw = set()
        for r in reads:
            add(r.w)
            if r.w is not None:
                raw.add(id(r.w))
        for w in writes:
            add(w.w)
            for x in w.r:
                add(x)
        final = []
        for d in deps:
            if d.eng == ins.eng and d.kind == "c" and ins.kind == "c":
                if ins.eng == "pe":
                    continue
                if id(d) not in raw:
                    continue
            final.append(d)
        for d in final:
            d.need = True
        ins.deps = final
        for r in reads:
            r.r.append(ins)
        for w in writes:
            w.w = ins
            w.r = []
        self.streams[ins.eng].append(ins)
        return ins

    def op(self, eng, fn, reads=(), writes=()):
        return self._add(Ins(eng, fn, "c"), list(reads), list(writes))

    def dma(self, eng, out, in_, key, reads=(), writes=()):
        ins = Ins(eng, lambda e: e.dma_start(out=out, in_=in_), "d", key)
        c = self.dma_cnt.get(key, 0) + 1
        self.dma_cnt[key] = c
        ins.sig = (("D", key), c * 16, c)
        return self._add(ins, list(reads), list(writes))

    def cc(self, in_t, out_t, reads=(), writes=()):
        idx = self.ncc
        self.ncc += 1
        ins = Ins("pool", lambda e: e.collective_compute(
            "AllGather", ALU.bypass, replica_groups=PAIRS,
            ins=[in_t.ap().opt()], outs=[out_t.ap().opt()]), "cc", idx)
        ins.sig = (("C", idx), 1, 1)
        return self._add(ins, list(reads), list(writes))

    def barrier(self, new, old):
        acc = []
        seen = set()
        for o in old:
            for x in ([o.w] if o.w is not None else []) + o.r:
                if id(x) not in seen:
                    seen.add(id(x))
                    acc.append(x)
        for n in new:
            n.w = None
            n.r = list(acc)

    def emit(self, nc, block, stack):
        nsig = {}
        for eng, lst in self.streams.items():
            k = 0
            for ins in lst:
                if ins.kind == "c" and ins.need:
                    k += 1
                    ins.sig = (("E", eng, (k - 1) // ROT), (k - 1) % ROT + 1, k)
            nsig[eng] = k
        sems = {}
        for eng, k in nsig.items():
            for j in range((k + ROT - 1) // ROT):
                sems[("E", eng, j)] = stack.enter_context(nc.semaphore(f"s_{eng}{j}"))
        for key in self.dma_cnt:
            sems[("D", key)] = stack.enter_context(nc.semaphore(f"d_{key}"))
        for i in range(self.ncc):
            sems[("C", i)] = stack.enter_context(nc.semaphore(f"c_{i}"))

        def run(eng, e):
            known = {}
            for ins in self.streams[eng]:
                for d in ins.deps:
                    semkey, val, gidx = d.sig
                    cls = semkey[:2] if semkey[0] == "E" else semkey
                    if known.get(cls, 0) >= gidx:
                        continue
                    known[cls] = gidx
                    e.wait_ge(sems[semkey], val)
                bi = ins.fn(e)
                if ins.kind == "c":
                    if ins.need:
                        bi.then_inc(sems[ins.sig[0]], 1)
                elif ins.kind == "d":
                    bi.then_inc(sems[ins.sig[0]], 16)
                else:
                    bi.then_inc(sems[ins.sig[0]])
            for key, c in self.dma_cnt.items():
                pass

        self._run = run
        self._sems = sems

        @block.tensor
        def _(e):
            run("pe", e)

        @block.scalar
        def _(e):
            run("act", e)

        @block.vector
        def _(e):
            run("dve", e)

        @block.gpsimd
        def _(e):
            run("pool", e)

        @block.sync
        def _(e):
            run("sp", e)
            for key, c in self.dma_cnt.items():
                e.wait_ge(sems[("D", key)], c * 16)


def build_program(debug=False):
    nc = bass.Bass("TRN2", target_bir_lowering=False)
    P = Prog()

    def dram_in(name, shape, dt=F32):
        return nc.dram_tensor(name, list(shape), dt, kind="ExternalInput")

    _rc = {}

    def get_role(e):
        if "r" not in _rc:
            _rc["r"] = e.partition_id() % 2
        return _rc["r"]

    x_d = dram_in("x", [NT, DM]).ap()
    vec_d = dram_in("vec", [128, NV]).ap()
    cst_d = dram_in("cst", [128, 6 * 128]).ap()
    rolec_d = dram_in("rolec", [2, 128, 2]).ap()
    W = {}
    for l in range(2):
        for f in (1, 2):
            W[f"f{f}w13_{l}"] = dram_in(f"f{f}w13_{l}", [DM, 2 * DFF]).ap()
            W[f"f{f}w2_{l}"] = dram_in(f"f{f}w2_{l}", [DFF, DM]).ap()
    W["a_w_in"] = dram_in("a_w_in", [DM, 2048]).ap()
    W["a_w_r"] = dram_in("a_w_r", [4, 256, 256]).ap()
    W["a_w_i"] = dram_in("a_w_i", [4, 256, 256]).ap()
    W["a_w_out"] = dram_in("a_w_out", [DM, DM]).ap()
    W["w_kv"] = dram_in("w_kv", [DM, 2048]).ap()
    W["b_w_q"] = dram_in("b_w_q", [DM, DM]).ap()
    W["b_w_o"] = dram_in("b_w_o", [DM, DM]).ap()
    y_d = nc.dram_tensor("y", [NT, DM], F32, kind="ExternalOutput").ap()

    hh_c = nc.dram_tensor("hh_c", [128, 12], F32)
    hh_g = nc.dram_tensor("hh_g", [256, 12], F32)
    st_c = nc.dram_tensor("st_c", [128, 8], F32)
    st_g = nc.dram_tensor("st_g", [256, 8], F32)
    a_sc = nc.dram_tensor("a_sc", [128, 8, NT], F32, kind=("ExternalOutput" if debug else "Internal"))
    u_sc = nc.dram_tensor("u_sc", [128, 8, NT], F32, kind=("ExternalOutput" if debug else "Internal"))
    kt_c = [nc.dram_tensor(f"kt_c{i}", [512, NT // 2], F32) for i in range(2)]
    kt_g = [nc.dram_tensor(f"kt_g{i}", [1024, NT // 2], F32) for i in range(2)]
    qt_c = [nc.dram_tensor(f"qt_c{i}", [512, NT // 2], F32) for i in range(2)]
    qt_g = [nc.dram_tensor(f"qt_g{i}", [1024, NT // 2], F32) for i in range(2)]
    v_c = [nc.dram_tensor(f"v_c{i}", [NT, 256], F32) for i in range(2)]
    v_g = [nc.dram_tensor(f"v_g{i}", [2 * NT, 256], F32) for i in range(2)]
    ot_c = [nc.dram_tensor(f"ot_c{i}", [256, NT], F32) for i in range(2)]
    ot_g = [nc.dram_tensor(f"ot_g{i}", [512, NT], F32) for i in range(2)]

    def bfv(t):
        return t.ap().bitcast(BF16)

    stack = ExitStack()
    ARENA_WORDS = 52864
    arena = stack.enter_context(nc.sbuf_tensor("arena", [128, ARENA_WORDS], F32))
    psum = stack.enter_context(nc.psum_tensor("ps", [128, 8, 512], F32))

    def view(off, shape, dt):
        n = int(np.prod(shape))
        nbytes = n * (4 if dt == F32 else 2)
        assert off % 4 == 0 and nbytes % 4 == 0
        ap = arena[:, off // 4:(off + nbytes) // 4]
        if dt != F32:
            ap = ap.bitcast(dt)
        if len(shape) == 2:
            ap = ap.rearrange("p (a b) -> p a b", a=shape[0])
        elif len(shape) == 3:
            ap = ap.rearrange("p (a b c) -> p a b c", a=shape[0], b=shape[1])
        return ap

    O_XT = 0
    O_CST = 65536
    O_HT = O_CST + 4096
    O_PH = O_HT + 32768
    xT = view(O_XT, [KC, NT], F32)
    cbf = view(O_CST, [6, 128], BF16)
    identf = view(O_CST + 1536, [128], F32)
    vec = view(O_CST + 2048, [NV], F32)
    flag = view(O_CST + 2560, [2], F32)
    misc = view(O_CST + 2568, [64], F32)
    hT = view(O_HT, [KC, NT], BF16)
    IDB, NTRI, NONES, CMASK, BD64, ONES1K = range(6)

    R_xT = [[Res(f"xT{c}_{t}") for t in range(NTT)] for c in range(KC)]
    R_hT = [Res(f"hT{t}") for t in range(NTT)]
    R_cst = Res("cst")
    R_bank = [Res(f"bank{i}") for i in range(8)]

    O_WS13 = O_PH
    O_WS2 = O_WS13 + 2 * 16384
    O_WB13 = O_WS2 + 2 * 8192
    O_WB2 = O_WB13 + 2 * 8192
    O_REST = O_WB2 + 2 * 4096
    ws13 = [view(O_WS13 + s * 16384, [KC, 2, 256], F32) for s in range(2)]
    ws2 = [view(O_WS2 + s * 8192, [2, 1024], F32) for s in range(2)]
    wb13 = [view(O_WB13 + s * 8192, [KC, 2, 256], BF16) for s in range(2)]
    wb2 = [view(O_WB2 + s * 4096, [2, 1024], BF16) for s in range(2)]
    R_ws13p = [[Res(f"ws13_{s}_{k}") for k in range(4)] for s in range(2)]
    R_ws13 = [Res(f"ws13_{s}") for s in range(2)]
    R_ws13a = [[R_ws13[s]] + R_ws13p[s] for s in range(2)]
    R_ws2 = [Res(f"ws2_{s}") for s in range(2)]
    R_wb13 = [[Res(f"wb13_{s}_{k}") for k in range(4)] for s in range(2)]
    R_wb2 = [Res(f"wb2_{s}") for s in range(2)]
    Abuf = [view(O_REST + s * 8192, [2, NT], BF16) for s in range(2)]
    R_A = [[Res(f"A{s}_{t}") for t in range(NTT)] for s in range(2)]
    sq = view(O_REST + 16384, [KC, TT], BF16)
    R_sq = Res("sq")
    rstd = view(O_REST + 24576, [TT], F32)
    R_rstd = Res("rstd")
    sg = [view(O_REST + 26624 + s * 1024, [TT], BF16) for s in range(2)]
    R_sg = [Res(f"sg{s}") for s in range(2)]
    assert O_REST + 28672 <= ARENA_WORDS * 4

    wunit = [0]

    cstage = view(O_WS13, [6, 128], F32)
    R_cstage = Res("cstage")
    P.dma("sp", cstage, cst_d.rearrange("p (a b) -> p a b", a=6), "misc0", writes=[R_cstage])
    P.dma("sp", vec, vec_d, "misc1", writes=[R_cst])
    def _role_dma(e):
        role = get_role(e)
        return e.dma_start(out=flag, in_=rolec_d[bass.ds(role, 1), :, :].rearrange("a p b -> p (a b)"))
    ins = Ins("sp", _role_dma, "d", "misc2")
    c = P.dma_cnt.get("misc2", 0) + 1
    P.dma_cnt["misc2"] = c
    ins.sig = (("D", "misc2"), c * 16, c)
    P._add(ins, [], [R_cst])
    P.op("dve", lambda e: e.tensor_copy(out=cbf, in_=cstage), reads=[R_cstage], writes=[R_cst])
    P.op("dve", lambda e: e.tensor_copy(out=identf, in_=cstage[:, 0, :]), reads=[R_cstage], writes=[R_cst])
    P.barrier(R_ws13 + R_ws13p[0] + R_ws13p[1], [R_cstage])

    xst = [view(O_WS13 + s * 16384, [4, DM], F32) for s in range(2)]
    R_xst = R_ws13a
    x_v = x_d.rearrange("(t j p) n -> t p j n", j=4, p=128)
    R_xj = [[Res(f"xst{s}_{j}") for j in range(4)] for s in range(2)]
    P.barrier(R_xj[0] + R_xj[1], R_ws13a[0] + R_ws13a[1])
    for tt in range(NTT):
        s = tt % 2
        for j in range(4):
            P.dma("sp", xst[s][:, j, :], x_v[tt][:, j, :], f"xst{s}_{j}", writes=[R_xj[s][j]])
        for c in range(KC):
            b = c % 4
            for j in range(4):
                P.op("pe", lambda e, b=b, j=j, s=s, c=c: e.transpose(
                    psum[:, b, j * 128:(j + 1) * 128], xst[s][:, j, c * 128:(c + 1) * 128], identf),
                    reads=[R_xj[s][j], R_cst], writes=[R_bank[b]])
            eng = "act" if c % 2 == 0 else "dve"
            if eng == "act":
                P.op("act", lambda e, b=b, c=c, tt=tt: e.copy(out=xT[:, c, tt * TT:(tt + 1) * TT], in_=psum[:, b, :]),
                     reads=[R_bank[b]], writes=[R_xT[c][tt]])
            else:
                P.op("dve", lambda e, b=b, c=c, tt=tt: e.tensor_copy(out=xT[:, c, tt * TT:(tt + 1) * TT], in_=psum[:, b, :]),
                     reads=[R_bank[b]], writes=[R_xT[c][tt]])

    P.barrier(R_ws13a[0] + R_ws13a[1], R_xj[0] + R_xj[1])

    def rmsnorm(tt, gcol):
        cols = slice(tt * TT, (tt + 1) * TT)
        P.op("act", lambda e: e.activation(out=sq, in_=xT[:, :, cols], func=AF.Square),
             reads=[R_xT[c][tt] for c in range(KC)], writes=[R_sq])
        for c in range(KC):
            P.op("pe", lambda e, c=c: e.matmul(psum[:, 7, :], cbf[:, ONES1K, :], sq[:, c, :], start=(c == 0), stop=(c == KC - 1)),
                 reads=[R_sq, R_cst], writes=[R_bank[7]])
        P.op("act", lambda e: e.activation(out=rstd, in_=psum[:, 7, :], func=AF.Ln, bias=EPS),
             reads=[R_bank[7]], writes=[R_rstd])
        P.op("act", lambda e: e.activation(out=rstd, in_=rstd, func=AF.Exp, scale=-0.5), reads=[R_rstd], writes=[R_rstd])
        for c in range(KC):
            P.op("dve", lambda e, c=c: e.scalar_tensor_tensor(out=hT[:, c, cols], in0=xT[:, c, cols], scalar=vec[:, gcol + c:gcol + c + 1],
                                                              in1=rstd, op0=ALU.mult, op1=ALU.mult),
                 reads=[R_xT[c][tt], R_rstd, R_cst], writes=[R_hT[tt]])

    def ffn(w13, w2, gcol):
        for tt in range(NTT):
            rmsnorm(tt, gcol)
        w13v = w13.rearrange("(kc p) n -> p kc n", p=128)
        pyb = [0]
        pending = []
        for u in range(NU):
            s = wunit[0] % 2
            wunit[0] += 1
            j0 = u * 256
            if u == 0:
                for ci in range(2):
                    for gu in range(2):
                        k = gu * 2 + ci
                        P.dma("sp", ws13[s][:, :, gu, ci * 128:(ci + 1) * 128], w13v[:, :, gu * DFF + j0 + ci * 128:gu * DFF + j0 + (ci + 1) * 128],
                              f"ws13_{s}_p{k}", writes=[R_ws13p[s][k]])
                        P.op("pool", lambda e, s=s, gu=gu, ci=ci: e.tensor_copy(out=wb13[s][:, :, gu, ci * 128:(ci + 1) * 128],
                                                                                 in_=ws13[s][:, :, gu, ci * 128:(ci + 1) * 128]),
                             reads=[R_ws13p[s][k]], writes=[R_wb13[s][k]])
            else:
                P.dma("sp", ws13[s][:, :, 0, :], w13v[:, :, j0:j0 + 256], f"ws13_{s}", writes=[R_ws13p[s][0], R_ws13p[s][1]])
                P.dma("sp", ws13[s][:, :, 1, :], w13v[:, :, DFF + j0:DFF + j0 + 256], f"ws13_{s}", writes=[R_ws13p[s][2], R_ws13p[s][3]])
                P.op("pool", lambda e, s=s: e.tensor_copy(out=wb13[s], in_=ws13[s]), reads=R_ws13p[s], writes=R_wb13[s])
            P.dma("sp", ws2[s], w2[j0:j0 + 256, :].rearrange("(c p) n -> p c n", p=128), f"ws2_{s}", writes=[R_ws2[s]])
            P.op("pool", lambda e, s=s: e.tensor_copy(out=wb2[s], in_=ws2[s]), reads=[R_ws2[s]], writes=[R_wb2[s]])
            a = u % 2
            prev_groups = pending
            gi = 0
            for tt in range(NTT):
                cols = slice(tt * TT, (tt + 1) * TT)
                for ci in range(2):
                    pb = (tt * 2 + ci) % 2
                    bg, bu = 2 * pb, 2 * pb + 1
                    for gu, bk in ((0, bg), (1, bu)):
                        for kc in range(KC):
                            P.op("pe", lambda e, s=s, kc=kc, gu=gu, ci=ci, bk=bk, cols=cols: e.matmul(
                                psum[:, bk, :], wb13[s][:, kc, gu, ci * 128:(ci + 1) * 128], hT[:, kc, cols],
                                start=(kc == 0), stop=(kc == KC - 1)),
                                reads=[R_wb13[s][gu * 2 + ci], R_hT[tt]], writes=[R_bank[bk]])
                    P.op("act", lambda e, pb=pb, bg=bg: e.activation(out=sg[pb], in_=psum[:, bg, :], func=AF.Silu),
                         reads=[R_bank[bg]], writes=[R_sg[pb]])
                    P.op("dve", lambda e, pb=pb, bu=bu, a=a, ci=ci, cols=cols: e.tensor_tensor(
                        out=Abuf[a][:, ci, cols], in0=sg[pb], in1=psum[:, bu, :], op=ALU.mult),
                        reads=[R_sg[pb], R_bank[bu]], writes=[R_A[a][tt]])
                    for _ in range(4):
                        if gi < len(prev_groups):
                            prev_groups[gi]()
                            gi += 1

            def mk_group(s, a, tt, dc):
                def emit_group():
                    cols = slice(tt * TT, (tt + 1) * TT)
                    bk = 4 + pyb[0] % 4
                    pyb[0] += 1
                    for ci in range(2):
                        P.op("pe", lambda e, ci=ci, bk=bk: e.matmul(
                            psum[:, bk, :], wb2[s][:, ci, dc * 128:(dc + 1) * 128], Abuf[a][:, ci, cols],
                            start=(ci == 0), stop=(ci == 1)),
                            reads=[R_wb2[s], R_A[a][tt]], writes=[R_bank[bk]])
                    P.op("dve", lambda e, bk=bk: e.scalar_tensor_tensor(
                        out=xT[:, dc, cols], in0=psum[:, bk, :], scalar=0.5, in1=xT[:, dc, cols], op0=ALU.mult, op1=ALU.add),
                        reads=[R_bank[bk], R_xT[dc][tt]], writes=[R_xT[dc][tt]])
                return emit_group
            pending = [mk_group(s, a, tt, dc) for tt in range(NTT) for dc in range(KC)]
        for g_ in pending:
            g_()

    def dma_fn(eng, fn, key, reads=(), writes=()):
        ins = Ins(eng, fn, "d", key)
        c = P.dma_cnt.get(key, 0) + 1
        P.dma_cnt[key] = c
        ins.sig = (("D", key), c * 16, c)
        return P._add(ins, list(reads), list(writes))

    ALL_FFN = R_ws13 + R_ws13p[0] + R_ws13p[1] + R_ws2 + R_wb13[0] + R_wb13[1] + R_wb2 + R_A[0] + R_A[1] + [R_sq, R_rstd] + R_sg

    def store_out(y_ap, key):
        ost = [view(O_WS13 + s * 16384, [4, DM], F32) for s in range(2)]
        y_v = y_ap.rearrange("(t j p) n -> t p j n", j=4, p=128)
        R_oj = [[Res(f"ostj{s}_{j}") for j in range(4)] for s in range(2)]
        P.barrier(R_oj[0] + R_oj[1], R_ws13a[0] + R_ws13a[1])
        k = 0
        for tt in range(NTT):
            s = tt % 2
            for j in range(4):
                for half in range(2):
                    b = k % 4
                    k += 1
                    for q in range(4):
                        c = half * 4 + q
                        P.op("pe", lambda e, b=b, q=q, c=c, tt=tt, j=j: e.transpose(
                            psum[:, b, q * 128:(q + 1) * 128], xT[:, c, tt * TT + j * 128: tt * TT + (j + 1) * 128], identf),
                            reads=[R_xT[c][tt], R_cst], writes=[R_bank[b]])
                    if half == 0:
                        P.op("act", lambda e, b=b, s=s, j=j: e.copy(out=ost[s][:, j, 0:512], in_=psum[:, b, :]),
                             reads=[R_bank[b]], writes=[R_oj[s][j]])
                    else:
                        P.op("dve", lambda e, b=b, s=s, j=j: e.tensor_copy(out=ost[s][:, j, 512:1024], in_=psum[:, b, :]),
                             reads=[R_bank[b]], writes=[R_oj[s][j]])
                P.dma("sp", y_v[tt][:, j, :], ost[s][:, j, :], f"{key}{s}_{j}", reads=[R_oj[s][j]])
        P.barrier(R_ws13a[0] + R_ws13a[1], R_oj[0] + R_oj[1])

    dbg_n = [0]

    def tap():
        if debug:
            dbg_n[0] += 1
            d = nc.dram_tensor(f"dbg{dbg_n[0]}", [NT, DM], F32, kind="ExternalOutput").ap()
            store_out(d, "out")

    P.op("act", lambda e: e.activation(out=misc[:, 0:8], in_=vec[:, V_LAM:V_LAM + 8], func=AF.Exp, scale=-1.0),
         reads=[R_cst], writes=[R_cst])
    P.op("act", lambda e: e.activation(out=misc[:, 0:8], in_=misc[:, 0:8], func=AF.Ln, bias=1.0),
         reads=[R_cst], writes=[R_cst])
    P.op("dve", lambda e: e.tensor_scalar(out=misc[:, 0:8], in0=misc[:, 0:8], scalar1=-8.0, scalar2=None, op0=ALU.mult),
         reads=[R_cst], writes=[R_cst])
    P.op("dve", lambda e: e.tensor_scalar(out=misc[:, 8:9], in0=vec[:, V_QG:V_QG + 1], scalar1=0.125, scalar2=None, op0=ALU.mult),
         reads=[R_cst], writes=[R_cst])

    ffn(W["f1w13_0"], W["f1w2_0"], V_F1N0)
    tap()

    def rglru():
        for tt in range(NTT):
            rmsnorm(tt, V_MIX0)
        o = O_PH
        rws = view(o, [KC, 256], F32); o += 8192
        rwb = view(o, [KC, 256], BF16); o += 4096
        gws = view(o, [2, 2, 256], F32); o += 4096
        gwb = view(o, [2, 2, 256], BF16); o += 2048
        rec = view(o, [2, 2052], F32); o += 16416
        xc = view(o, [2, NT], F32); o += 16384
        xcb = view(o, [2, NT], BF16); o += 8192
        rbs = [view(o + q_ * 8192, [NT], F32) for q_ in range(2)]; o += 16384
        ibs = [view(o + q_ * 8192, [NT], F32) for q_ in range(2)]; o += 16384
        tbs = [view(o + q_ * 8192, [NT], F32) for q_ in range(2)]; o += 16384
        osm = O_CST + 3072
        hraw = view(osm, [24], BF16); osm += 64
        hhal = view(osm, [KC, 16], BF16); osm += 256
        stc = view(osm, [8], F32); osm += 32
        sraw = view(osm, [8], F32); osm += 32
        init = view(osm, [8], F32); osm += 32
        assert osm <= O_CST + 4096
        assert o <= ARENA_WORDS * 4
        R_rws, R_rwb, R_gws, R_gwb = Res("rws"), Res("rwb"), Res("gws"), Res("gwb")
        R_rec = [Res("rec0"), Res("rec1")]
        R_xc = [Res("xc0"), Res("xc1")]
        R_xcb = [Res("xcb0"), Res("xcb1")]
        R_rs = [Res("r0"), Res("r1")]
        R_is = [Res("i0"), Res("i1")]
        R_ts = [Res("t0"), Res("t1")]
        R_small = Res("rg_small")
        R_hhc, R_hhg, R_stcD, R_stg = Res("hh_c"), Res("hh_g"), Res("st_c"), Res("st_g")
        R_asc = [Res(f"asc{c}") for c in range(KC)]
        p1 = [R_rws, R_rwb, R_gws, R_gwb] + R_rec + R_xc + R_xcb + R_rs + R_is + R_ts + [R_small]
        P.barrier(p1, ALL_FFN)
        w_in_v = W["a_w_in"].rearrange("(kc p) n -> p kc n", p=128)

        def load_rw(n):
            P.dma("sp", rws, w_in_v[:, :, 1024 + n * 256:1024 + (n + 1) * 256], "rws", writes=[R_rws])
            P.op("pool", lambda e: e.tensor_copy(out=rwb, in_=rws), reads=[R_rws], writes=[R_rwb])
        load_rw(0)
        P.dma("sp", bfv(hh_c).rearrange("p (c k) -> p c k", k=3), hT[:, :, NT - 3:NT], "rgx", reads=[R_hT[NTT - 1]], writes=[R_hhc])
        P.cc(hh_c, hh_g, reads=[R_hhc], writes=[R_hhg])
        P.dma("sp", hraw, bfv(hh_g)[0:128, :], "rgx", reads=[R_hhg], writes=[R_small])
        P.op("dve", lambda e: e.memset(hhal, 0.0), writes=[R_small])
        P.op("dve", lambda e: e.tensor_scalar(out=hhal[:, :, 0:3], in0=hraw.rearrange("p (c k) -> p c k", k=3), scalar1=flag[:, 0:1], scalar2=None, op0=ALU.mult),
             reads=[R_small, R_cst], writes=[R_small])
        bk = [0]

        def stage_a1(n):
            if n > 0:
                load_rw(n)
            for j in range(2):
                for tt in range(NTT):
                    b = bk[0] % 4
                    bk[0] += 1
                    for kc in range(KC):
                        P.op("pe", lambda e, kc=kc, j=j, b=b, tt=tt: e.matmul(psum[:, b, :], rwb[:, kc, j * 128:(j + 1) * 128],
                                                                            hT[:, kc, tt * TT:(tt + 1) * TT], start=(kc == 0), stop=(kc == KC - 1)),
                             reads=[R_rwb, R_hT[tt]], writes=[R_bank[b]])
                    P.op("act", lambda e, j=j, b=b, tt=tt: e.copy(out=rec[:, j, 3 + tt * TT:3 + (tt + 1) * TT], in_=psum[:, b, :]),
                         reads=[R_bank[b]], writes=[R_rec[j]])
            for j in range(2):
                for kc in range(KC):
                    P.op("pe", lambda e, kc=kc, j=j: e.matmul(psum[:, 7, 0:16], rwb[:, kc, j * 128:(j + 1) * 128], hhal[:, kc, :],
                                                            start=(kc == 0), stop=(kc == KC - 1)),
                         reads=[R_rwb, R_small], writes=[R_bank[7]])
                P.op("act", lambda e, j=j: e.copy(out=rec[:, j, 0:3], in_=psum[:, 7, 0:3]), reads=[R_bank[7]], writes=[R_rec[j]])

        def stage_a2(n):
            for j in range(2):
                ch = 2 * n + j
                P.op("dve", lambda e, j=j, ch=ch: e.tensor_scalar(out=xc[:, j, :], in0=rec[:, j, 3:3 + NT],
                                                                scalar1=vec[:, V_CONVW + 24 + ch:V_CONVW + 25 + ch],
                                                                scalar2=vec[:, V_CONVB + ch:V_CONVB + ch + 1], op0=ALU.mult, op1=ALU.add),
                     reads=[R_rec[j], R_cst], writes=[R_xc[j]])
                for k in range(3):
                    P.op("dve", lambda e, j=j, ch=ch, k=k: e.scalar_tensor_tensor(
                        out=xc[:, j, :], in0=rec[:, j, k:k + NT], scalar=vec[:, V_CONVW + 8 * k + ch:V_CONVW + 8 * k + ch + 1],
                        in1=xc[:, j, :], op0=ALU.mult, op1=ALU.add),
                        reads=[R_rec[j], R_xc[j], R_cst], writes=[R_xc[j]])
                P.op("act", lambda e, j=j: e.copy(out=xcb[:, j, :], in_=xc[:, j, :]), reads=[R_xc[j]], writes=[R_xcb[j]])

        def stage_b(n):
            P.dma("sp", gws[:, 0], W["a_w_r"][n].rearrange("(cc p) d -> p cc d", p=128), "gws", writes=[R_gws])
            P.dma("sp", gws[:, 1], W["a_w_i"][n].rearrange("(cc p) d -> p cc d", p=128), "gws", writes=[R_gws])
            P.op("pool", lambda e: e.tensor_copy(out=gwb, in_=gws), reads=[R_gws], writes=[R_gwb])
            for j in range(2):
                ch = 2 * n + j
                for gi, (buf, Rb, bcol) in enumerate(((rbs[j], R_rs[j], V_BR), (ibs[j], R_is[j], V_BI))):
                    for tt in range(NTT):
                        b = bk[0] % 4
                        bk[0] += 1
                        for cc in range(2):
                            P.op("pe", lambda e, gi=gi, cc=cc, j=j, b=b, tt=tt: e.matmul(
                                psum[:, b, :], gwb[:, gi, cc, j * 128:(j + 1) * 128], xcb[:, cc, tt * TT:(tt + 1) * TT],
                                start=(cc == 0), stop=(cc == 1)),
                                reads=[R_gwb, R_xcb[cc]], writes=[R_bank[b]])
                        P.op("act", lambda e, buf=buf, b=b, tt=tt, bcol=bcol, ch=ch: e.activation(
                            out=buf[:, tt * TT:(tt + 1) * TT], in_=psum[:, b, :], func=AF.Sigmoid, bias=vec[:, bcol + ch:bcol + ch + 1]),
                            reads=[R_bank[b], R_cst], writes=[Rb])
            for j in range(2):
                ch = 2 * n + j
                P.op("act", lambda e, ch=ch, rb=rbs[j]: e.activation(out=rb, in_=rb, func=AF.Exp, scale=misc[:, ch:ch + 1]),
                     reads=[R_rs[j], R_cst], writes=[R_rs[j]])
            for j in range(2):
                P.op("dve", lambda e, rb=rbs[j], tb_=tbs[j]: e.scalar_tensor_tensor(out=tb_, in0=rb, scalar=-1.0, in1=rb, op0=ALU.mult, op1=ALU.mult),
                     reads=[R_rs[j]], writes=[R_ts[j]])
            for j in range(2):
                P.op("act", lambda e, tb_=tbs[j]: e.activation(out=tb_, in_=tb_, func=AF.Sqrt, bias=1.0), reads=[R_ts[j]], writes=[R_ts[j]])
            for j in range(2):
                ch = 2 * n + j
                rb, ib, tb_ = rbs[j], ibs[j], tbs[j]
                R_r, R_i, R_t = R_rs[j], R_is[j], R_ts[j]
                P.op("dve", lambda e, ib=ib, tb_=tb_: e.tensor_tensor(out=ib, in0=ib, in1=tb_, op=ALU.mult), reads=[R_i, R_t], writes=[R_i])
                P.op("dve", lambda e, j=j, ib=ib: e.tensor_tensor(out=ib, in0=ib, in1=xc[:, j, :], op=ALU.mult), reads=[R_i, R_xc[j]], writes=[R_i])
                P.dma("sp", a_sc[:, ch, :], rb, f"rgsp{ch}", reads=[R_r], writes=[R_asc[ch]])
                P.dma("sp", u_sc[:, ch, :], ib, f"rgsp{ch}", reads=[R_i], writes=[R_asc[ch]])
                P.op("dve", lambda e, rb=rb, ib=ib, tb_=tb_: e.tensor_tensor_scan(out=tb_, data0=rb, data1=ib, initial=0.0, op0=ALU.mult, op1=ALU.add),
                     reads=[R_r, R_i, R_t], writes=[R_t])
                P.op("dve", lambda e, ch=ch, tb_=tb_: e.tensor_copy(out=stc[:, ch:ch + 1], in_=tb_[:, NT - 1:NT]), reads=[R_t], writes=[R_small])

        stage_a1(0)
        stage_a2(0)
        for n in range(4):
            if n + 1 < 4:
                stage_a1(n + 1)
            stage_b(n)
            if n + 1 < 4:
                stage_a2(n + 1)
        P.dma("sp", st_c.ap(), stc, "rgx", reads=[R_small], writes=[R_stcD])
        P.cc(st_c, st_g, reads=[R_stcD], writes=[R_stg])
        def load_init():
            P.dma("sp", sraw, st_g[0:128, :], "rgx", reads=[R_stg], writes=[R_small])
            P.op("dve", lambda e: e.tensor_scalar(out=init, in0=sraw, scalar1=flag[:, 0:1], scalar2=None, op0=ALU.mult),
                 reads=[R_small, R_cst], writes=[R_small])
        if debug:
            load_init()
            for nm, ap_, n in (("d_init", init, 8), ("d_flag", flag, 2), ("d_sraw", sraw, 8), ("d_stc", stc, 8)):
                dd = nc.dram_tensor(nm, [128, n], F32, kind="ExternalOutput").ap()
                P.dma("sp", dd, ap_, "out", reads=[R_small, R_cst])
        o = O_PH
        pws = [view(o + s * 8192, [KC, 256], F32) for s in range(2)]; o += 16384
        pwb = [view(o + s * 4096, [KC, 256], BF16) for s in range(2)]; o += 8192
        al = [view(o + s * 8192, [NT], F32) for s in range(2)]; o += 16384
        ul = [view(o + s * 8192, [NT], F32) for s in range(2)]; o += 16384
        hs = view(o, [NT], F32); o += 8192
        yT = view(o, [KC, NT], BF16); o += 32768
        gbt = [view(o + s * 2048, [TT], F32) for s in range(2)]; o += 4096
        assert o <= ARENA_WORDS * 4 - 256
        R_pws = [Res("pws0"), Res("pws1")]
        R_pwb = [Res("pwb0"), Res("pwb1")]
        R_al = [Res("al0"), Res("al1")]
        R_hs = Res("hs")
        R_yT = [Res(f"yT{t}") for t in range(NTT)]
        R_gbt = [Res("gbt0"), Res("gbt1")]
        p2 = R_pws + R_pwb + R_al + [R_hs] + R_yT + R_gbt
        P.barrier(p2, [x for x in p1 if x is not R_small])
        wu = [0]
        gb = [0]
        for n in range(4):
            s = wu[0] % 2
            wu[0] += 1
            P.dma("sp", pws[s], w_in_v[:, :, n * 256:(n + 1) * 256], f"pws{s}", writes=[R_pws[s]])
            P.op("pool", lambda e, s=s: e.tensor_copy(out=pwb[s], in_=pws[s]), reads=[R_pws[s]], writes=[R_pwb[s]])
            for j in range(2):
                ch = 2 * n + j
                pr = ch % 2
                P.dma("sp", al[pr], a_sc[:, ch, :], f"al{pr}", reads=[R_asc[ch]], writes=[R_al[pr]])
                P.dma("sp", ul[pr], u_sc[:, ch, :], f"al{pr}", reads=[R_asc[ch]], writes=[R_al[pr]])
                if ch == 0 and not debug:
                    load_init()
                P.op("dve", lambda e, pr=pr, ch=ch: e.scalar_tensor_tensor(out=ul[pr][:, 0:1], in0=al[pr][:, 0:1], scalar=init[:, ch:ch + 1],
                                                                          in1=ul[pr][:, 0:1], op0=ALU.mult, op1=ALU.add),
                     reads=[R_al[pr], R_small], writes=[R_al[pr]])
                P.op("dve", lambda e, pr=pr, ch=ch: e.tensor_tensor_scan(out=hs, data0=al[pr], data1=ul[pr], initial=0.0,
                                                                        op0=ALU.mult, op1=ALU.add),
                     reads=[R_al[pr], R_small], writes=[R_hs])
                for tt in range(NTT):
                    b = bk[0] % 4
                    bk[0] += 1
                    g2 = gb[0] % 2
                    gb[0] += 1
                    for kc in range(KC):
                        P.op("pe", lambda e, kc=kc, j=j, b=b, tt=tt, s=s: e.matmul(psum[:, b, :], pwb[s][:, kc, j * 128:(j + 1) * 128],
                                                                                 hT[:, kc, tt * TT:(tt + 1) * TT], start=(kc == 0), stop=(kc == KC - 1)),
                             reads=[R_pwb[s], R_hT[tt]], writes=[R_bank[b]])
                    P.op("act", lambda e, b=b, g2=g2: e.activation(out=gbt[g2], in_=psum[:, b, :], func=AF.Gelu_apprx_tanh),
                         reads=[R_bank[b]], writes=[R_gbt[g2]])
                    P.op("dve", lambda e, g2=g2, ch=ch, tt=tt: e.tensor_tensor(out=yT[:, ch, tt * TT:(tt + 1) * TT], in0=hs[:, tt * TT:(tt + 1) * TT],
                                                                              in1=gbt[g2], op=ALU.mult),
                         reads=[R_hs, R_gbt[g2]], writes=[R_yT[tt]])
        if debug:
            dd = nc.dram_tensor("d_yT", [128, KC, NT], BF16, kind="ExternalOutput").ap()
            P.dma("sp", dd, yT, "out", reads=R_yT)
            for nm, ap_, rr in (("d_al0", al[0], R_al[0]), ("d_al1", al[1], R_al[1]), ("d_ul0", ul[0], R_al[0]), ("d_ul1", ul[1], R_al[1]), ("d_hs", hs, R_hs)):
                dd = nc.dram_tensor(nm, [128, NT], F32, kind="ExternalOutput").ap()
                P.dma("sp", dd, ap_, "out", reads=[rr])
        proj_accum(W["a_w_out"], yT, R_yT, pws, pwb, R_pws, R_pwb, wu, "pws")
        return p2 + [R_small]

    def proj_accum(w, src, R_src, pws, pwb, R_pws, R_pwb, wu, keyp, pre_hook=None):
        wv = w.rearrange("(kc p) n -> p kc n", p=128)
        bk = 0
        for q in range(4):
            s = wu[0] % 2
            wu[0] += 1
            P.dma("sp", pws[s], wv[:, :, q * 256:(q + 1) * 256], f"{keyp}{s}", writes=[R_pws[s]])
            P.op("pool", lambda e, s=s: e.tensor_copy(out=pwb[s], in_=pws[s]), reads=[R_pws[s]], writes=[R_pwb[s]])
            if q == 0 and pre_hook is not None:
                pre_hook()
            for tt in range(NTT):
                for ci in range(2):
                    dc = 2 * q + ci
                    b = 4 + bk % 4
                    bk += 1
                    for kc in range(KC):
                        P.op("pe", lambda e, kc=kc, ci=ci, b=b, tt=tt, s=s: e.matmul(psum[:, b, :], pwb[s][:, kc, ci * 128:(ci + 1) * 128],
                                                                                  src[:, kc, tt * TT:(tt + 1) * TT], start=(kc == 0), stop=(kc == KC - 1)),
                             reads=[R_pwb[s]] + (list(R_src[tt]) if isinstance(R_src[tt], (list, tuple)) else [R_src[tt]]), writes=[R_bank[b]])
                    P.op("dve", lambda e, b=b, dc=dc, tt=tt: e.tensor_tensor(out=xT[:, dc, tt * TT:(tt + 1) * TT], in0=psum[:, b, :],
                                                                            in1=xT[:, dc, tt * TT:(tt + 1) * TT], op=ALU.add),
                         reads=[R_bank[b], R_xT[dc][tt]], writes=[R_xT[dc][tt]])

    prev = rglru()
    tap()
    P.barrier(ALL_FFN, prev)
    ffn(W["f2w13_0"], W["f2w2_0"], V_F2N0)
    tap()

    R_ktc, R_ktg, R_vc, R_vg, R_qtc, R_qtg, R_otc, R_otg = ([Res(n + "0"), Res(n + "1")] for n in ("ktc", "ktg", "vc", "vg", "qtc", "qtg", "otc", "otg"))

    def headproj(w, col0, gain_ap, dst, R_dst, gcol, with_v, mid_hook=None):
        for tt in range(NTT):
            rmsnorm(tt, gcol)
        o = O_PH
        kws = [view(o + s * 8192, [KC, 256], F32) for s in range(2)]; o += 16384
        kwb = [view(o + s * 4096, [KC, 256], BF16) for s in range(2)]; o += 8192
        sqk = [view(o + s * 1024, [TT], BF16) for s in range(3)]; o += 3072
        rsk = [view(o + s * 2048, [TT], F32) for s in range(3)]; o += 6144
        kst = [view(o + s * 4096, [NT], BF16) for s in range(2)]; o += 8192
        wvb = view(o, [KC, 1024], BF16); o += 16384
        vst = [view(o + s * 2048, [1024], BF16) for s in range(2)]; o += 4096
        vws = [view(o + s * 8192, [KC, 256], F32) for s in range(2)]; o += 16384
        R_vws = [Res("vws0"), Res("vws1")]
        assert o <= ARENA_WORDS * 4
        R_kws = [Res("kws0"), Res("kws1")]
        R_kwb = [Res("kwb0"), Res("kwb1")]
        R_sqk = [Res("sqk0"), Res("sqk1"), Res("sqk2")]
        R_rsk = [Res("rsk0"), Res("rsk1"), Res("rsk2")]
        tail = [None]
        R_kst = [Res("kst0"), Res("kst1")]
        R_wvb = Res("wvb")
        R_vst = [Res("vst0"), Res("vst1")]
        mine = R_kws + R_kwb + R_sqk + R_rsk + R_kst + [R_wvb] + R_vst + R_vws
        P.barrier(mine, ALL_FFN)
        wv = w.rearrange("(kc p) n -> p kc n", p=128)
        wu = 0
        bk = 0
        for qi, q in enumerate((0, 2, 1, 3)):
            if qi == 2 and mid_hook is not None:
                if tail[0] is not None:
                    tail[0]()
                    tail[0] = None
            s = wu % 2
            wu += 1
            P.dma("sp", kws[s], wv[:, :, col0 + q * 256:col0 + (q + 1) * 256], f"kws{s}", writes=[R_kws[s]])
            P.op("pool", lambda e, s=s: e.tensor_copy(out=kwb[s], in_=kws[s]), reads=[R_kws[s]], writes=[R_kwb[s]])
            if with_v:
                P.dma("sp", vws[qi % 2], wv[:, :, 1024 + qi * 256:1024 + (qi + 1) * 256], f"vws{qi % 2}", writes=[R_vws[qi % 2]])
                P.op("pool", lambda e, qi=qi: e.tensor_copy(out=wvb[:, :, qi * 256:(qi + 1) * 256], in_=vws[qi % 2]),
                     reads=[R_vws[qi % 2]], writes=[R_wvb])
            if qi == 3 and mid_hook is not None:
                mid_hook()
            for ci in range(2):
                oc = 2 * q + ci
                ks = oc % 2
                for tt in range(NTT):
                    b = bk % 4
                    t2 = bk % 3
                    bk += 1
                    for kc in range(KC):
                        P.op("pe", lambda e, kc=kc, ci=ci, b=b, tt=tt, s=s: e.matmul(psum[:, b, :], kwb[s][:, kc, ci * 128:(ci + 1) * 128],
                                                                                  hT[:, kc, tt * TT:(tt + 1) * TT], start=(kc == 0), stop=(kc == KC - 1)),
                             reads=[R_kwb[s], R_hT[tt]], writes=[R_bank[b]])
                    P.op("act", lambda e, b=b, t2=t2: e.activation(out=sqk[t2], in_=psum[:, b, :], func=AF.Square),
                         reads=[R_bank[b]], writes=[R_sqk[t2]])
                    if tail[0] is not None:
                        tail[0]()

                    def mk_tail(b=b, t2=t2, ks=ks, tt=tt, oc=oc, last=(tt == NTT - 1)):
                        def emit_tail():
                            mb = 4 + t2
                            P.op("pe", lambda e: e.matmul(psum[:, mb, :], cbf[:, BD64, :], sqk[t2], start=True, stop=True),
                                 reads=[R_sqk[t2], R_cst], writes=[R_bank[mb]])
                            P.op("act", lambda e: e.activation(out=rsk[t2], in_=psum[:, mb, :], func=AF.Ln, bias=EPS),
                                 reads=[R_bank[mb]], writes=[R_rsk[t2]])
                            P.op("act", lambda e: e.activation(out=rsk[t2], in_=rsk[t2], func=AF.Exp, scale=-0.5), reads=[R_rsk[t2]], writes=[R_rsk[t2]])
                            P.op("dve", lambda e: e.scalar_tensor_tensor(
                                out=kst[ks][:, tt * TT:(tt + 1) * TT], in0=psum[:, b, :], scalar=gain_ap, in1=rsk[t2], op0=ALU.mult, op1=ALU.mult),
                                reads=[R_bank[b], R_rsk[t2], R_cst], writes=[R_kst[ks]])
                            if last:
                                r0 = (oc // 4) * 256 + (oc % 2) * 128
                                P.dma("sp", bfv(dst[(oc % 4) // 2])[r0:r0 + 128, :], kst[ks], f"kst{ks}", reads=[R_kst[ks]], writes=[R_dst[(oc % 4) // 2]])
                        return emit_tail
                    tail[0] = mk_tail()
        if tail[0] is not None:
            tail[0]()
            tail[0] = None
        if with_v:
            for tb in range(16):
                vs = tb % 2
                tt = tb // 4
                for hf in range(2):
                    b = bk % 3
                    bk += 1
                    for kc in range(KC):
                        P.op("pe", lambda e, kc=kc, b=b, tb=tb, hf=hf: e.matmul(psum[:, b, :], hT[:, kc, tb * 128:(tb + 1) * 128],
                                                                              wvb[:, kc, hf * 512:(hf + 1) * 512], start=(kc == 0), stop=(kc == KC - 1)),
                             reads=[R_wvb, R_hT[tt]], writes=[R_bank[b]])
                    if hf == 0:
                        P.op("act", lambda e, b=b, vs=vs: e.copy(out=vst[vs][:, 0:512], in_=psum[:, b, :]), reads=[R_bank[b]], writes=[R_vst[vs]])
                    else:
                        P.op("dve", lambda e, b=b, vs=vs: e.tensor_copy(out=vst[vs][:, 512:1024], in_=psum[:, b, :]), reads=[R_bank[b]], writes=[R_vst[vs]])
                for i in range(2):
                    P.dma("sp", bfv(v_c[i])[tb * 128:(tb + 1) * 128, :].rearrange("p (g f) -> p g f", g=2),
                          vst[vs].rearrange("p (g i f) -> p g i f", g=2, i=2)[:, :, i, :], f"vst{vs}", reads=[R_vst[vs]], writes=[R_vc[i]])
        return mine

    prev = headproj(W["w_kv"], 0, vec[:, V_KG:V_KG + 1], kt_c, R_ktc, V_KVN, True,
                    mid_hook=lambda: P.cc(kt_c[0], kt_g[0], reads=[R_ktc[0]], writes=[R_ktg[0]]))
    P.cc(kt_c[1], kt_g[1], reads=[R_ktc[1]], writes=[R_ktg[1]])
    for i in range(2):
        P.cc(v_c[i], v_g[i], reads=[R_vc[i]], writes=[R_vg[i]])

    P.barrier(ALL_FFN, prev)
    ffn(W["f1w13_1"], W["f1w2_1"], V_F1N1)
    tap()

    prev = headproj(W["b_w_q"], 0, misc[:, 8:9], qt_c, R_qtc, V_MIX1, False,
                    mid_hook=lambda: P.cc(qt_c[0], qt_g[0], reads=[R_qtc[0]], writes=[R_qtg[0]]))
    P.cc(qt_c[1], qt_g[1], reads=[R_qtc[1]], writes=[R_qtg[1]])

    def attention():
        o = O_PH
        KT = [view(o + s * 8192, [4096], BF16) for s in range(2)]; o += 16384
        QT = [view(o + s * 8192, [4096], BF16) for s in range(2)]; o += 16384
        Vall = view(o, [32, 512], BF16); o += 32768
        ebuf = view(o, [4, TT], F32); o += 8192
        spt = view(o, [4, TT], BF16); o += 4096
        Sx = view(o, [4, TT], BF16); o += 4096
        xcb = view(o, [3, TT], F32); o += 6144
        wt = view(o, [4, TT], BF16); o += 4096
        ost = view(o, [2, TT], BF16); o += 2048
        zer = view(o, [TT], BF16); o += 1024
        assert o <= ARENA_WORDS * 4
        R_KT = [Res("KT0"), Res("KT1")]
        R_V = Res("Vall")
        R_e = [Res(f"e{i}") for i in range(4)]
        R_sp = [Res(f"sp{i}") for i in range(4)]
        R_S = [Res(f"S{i}") for i in range(4)]
        R_xc = [Res(f"xc{i}") for i in range(3)]
        R_wt = [Res(f"wt{i}") for i in range(4)]
        R_ost = [Res("ost0"), Res("ost1")]
        R_zer = Res("zer")
        mine = R_KT + [R_V] + R_e + R_sp + R_S + R_xc + R_wt + R_ost + [R_zer]
        P.barrier(mine, prev)
        P.op("pool", lambda e: e.memset(zer, 0.0), writes=[R_zer])
        for s_ in range(2):
            P.op("pool", lambda e, s_=s_: e.memset(KT[s_][64:128, :], 0.0), writes=[R_KT[s_]])
            P.op("pool", lambda e, s_=s_: e.memset(QT[s_][64:128, :], 0.0), writes=[R_KT[s_]])
        kt3 = [bfv(kt_g[i]).rearrange("(s2 rr) t -> rr s2 t", s2=2) for i in range(2)]
        qt3 = [bfv(qt_g[i]).rearrange("(s2 rr) t -> rr s2 t", s2=2) for i in range(2)]
        v3 = [bfv(v_g[i]).rearrange("(b p) f -> p b f", p=128) for i in range(2)]

        def load_v():
            for i in range(2):
                def ld_v(e, i=i):
                    role = get_role(e)
                    return e.dma_start(out=Vall[:, :, i * 256:(i + 1) * 256], in_=v3[i][:, :, bass.ds(role * 256, 256)])
                dma_fn("sp", ld_v, "vall", reads=[R_vg[i]], writes=[R_V])

        tiles = []
        ngrp = 0
        for hl in range(8):
            for g in range(8):
                ob = 5 + ngrp % 2
                ngrp += 1
                kbs = list(range(4 * g + 3, -1, -1))
                for n_, kb in enumerate(kbs):
                    tiles.append(dict(hl=hl, g=g, kb=kb, c0=max(0, kb - 4 * g) * 128, first=(n_ == 0), last=(n_ == len(kbs) - 1),
                                      ob=ob, diag=(kb >= 4 * g), gi=ngrp - 1))
        N = len(tiles)
        loaded = set()

        def load_head(hl):
            if hl in loaded or hl >= 8:
                return
            loaded.add(hl)
            s = hl % 2

            def ld_k(e, s=s, hl=hl):
                role = get_role(e)
                return e.dma_start(out=KT[s][0:64, :].rearrange("p (a t) -> p a t", a=2),
                                   in_=kt3[hl // 4][bass.ds((role * 256 + (hl % 4) * 64) if hl % 4 else role * 256, 64), :, :])

            def ld_q(e, s=s, hl=hl):
                role = get_role(e)
                return e.dma_start(out=QT[s][0:64, :].rearrange("p (a t) -> p a t", a=2),
                                   in_=qt3[hl // 4][bass.ds((role * 256 + (hl % 4) * 64) if hl % 4 else role * 256, 64), :, :])
            dma_fn("sp", ld_k, f"kq{s}", reads=[R_ktg[hl // 4]], writes=[R_KT[s]])
            dma_fn("sp", ld_q, f"kq{s}", reads=[R_qtg[hl // 4]], writes=[R_KT[s]])

        def st_z(t):
            T = tiles[t]
            load_head(T["hl"])
            load_head(T["hl"] + 1)
            s, kb, c0, g, z = T["hl"] % 2, T["kb"], T["c0"], T["g"], t % 2
            P.op("pe", lambda e: e.matmul(psum[:, z, c0:TT], KT[s][:, kb * 128:(kb + 1) * 128], QT[s][:, g * TT + c0:(g + 1) * TT],
                                          start=True, stop=True), reads=[R_KT[s]], writes=[R_bank[z]])
            P.op("pe", lambda e: e.matmul(psum[:, 7, :], cbf[:, NONES, :], zer, start=True, stop=True),
                 reads=[R_zer, R_cst], writes=[R_bank[7]])

        def st_prep(t):
            T = tiles[t]
            r, c0 = t % 4, T["c0"]
            if T["first"]:
                P.op("dve", lambda e: e.memset(Sx[:, r, :], 0.0), writes=[R_S[r]])

        def st_e(t):
            T = tiles[t]
            r, c0, z = t % 4, T["c0"], t % 2
            P.op("act", lambda e: e.activation(out=ebuf[:, r, c0:TT], in_=psum[:, z, c0:TT], func=AF.Exp),
                 reads=[R_bank[z]], writes=[R_e[r]])

        def st_sp(t):
            T = tiles[t]
            r, c0 = t % 4, T["c0"]
            P.op("act", lambda e: e.activation(out=spt[:, r, c0:TT], in_=ebuf[:, r, c0:TT], func=AF.Ln, bias=1.0),
                 reads=[R_e[r]], writes=[R_sp[r]])
            if T["diag"]:
                P.op("dve", lambda e: e.tensor_tensor(out=spt[:, r, c0:c0 + 128], in0=spt[:, r, c0:c0 + 128], in1=cbf[:, CMASK, :], op=ALU.mult),
                     reads=[R_sp[r], R_cst], writes=[R_sp[r]])
            rn = (t + 1) % 4
            if c0 > 0:
                P.op("dve", lambda e: e.memset(Sx[:, rn, 0:c0], 0.0), writes=[R_S[rn]])
            P.op("dve", lambda e: e.tensor_tensor(out=Sx[:, rn, c0:TT], in0=Sx[:, r, c0:TT], in1=spt[:, r, c0:TT], op=ALU.add),
                 reads=[R_S[r], R_sp[r]], writes=[R_S[rn]])

        def st_c(t):
            T = tiles[t]
            r, c0, l, first = t % 4, T["c0"], 2 + t % 3, T["first"]
            P.op("pe", lambda e: e.matmul(psum[:, l, c0:TT], cbf[:, NTRI, :], spt[:, r, c0:TT], start=True, stop=first),
                 reads=[R_sp[r], R_cst], writes=[R_bank[l]])
            if not first:
                P.op("pe", lambda e: e.matmul(psum[:, l, c0:TT], cbf[:, NONES, :], Sx[:, r, c0:TT], start=False, stop=True),
                     reads=[R_S[r], R_cst], writes=[R_bank[l]])

        def st_x(t):
            T = tiles[t]
            r, c0, l, x_ = t % 4, T["c0"], 2 + t % 3, t % 3
            P.op("act", lambda e: e.activation(out=xcb[:, x_, c0:TT], in_=psum[:, l, c0:TT], func=AF.Exp),
                 reads=[R_bank[l]], writes=[R_xc[x_]])
            P.op("dve", lambda e: e.tensor_tensor(out=wt[:, r, c0:TT], in0=xcb[:, x_, c0:TT], in1=ebuf[:, r, c0:TT], op=ALU.mult),
                 reads=[R_xc[x_], R_e[r]], writes=[R_wt[r]])
            if T["diag"]:
                P.op("dve", lambda e: e.tensor_tensor(out=wt[:, r, c0:c0 + 128], in0=wt[:, r, c0:c0 + 128], in1=cbf[:, CMASK, :], op=ALU.mult),
                     reads=[R_wt[r], R_cst], writes=[R_wt[r]])

        def st_v(t):
            T = tiles[t]
            r, c0, ob, hl, kb, g = t % 4, T["c0"], T["ob"], T["hl"], T["kb"], T["g"]
            vsl = slice((hl // 2) * 128, (hl // 2 + 1) * 128)
            if T["first"]:
                P.op("pe", lambda e: e.matmul(psum[:, ob, :], Vall[:, 0, vsl], zer, start=True, stop=False),
                     reads=[R_V, R_zer], writes=[R_bank[ob]])
            last = T["last"]
            P.op("pe", lambda e: e.matmul(psum[:, ob, c0:TT], Vall[:, kb, vsl], wt[:, r, c0:TT], start=False, stop=last),
                 reads=[R_V, R_wt[r]], writes=[R_bank[ob]])
            if last:
                op_ = T["gi"] % 2
                hp = (hl % 2) * 64
                P.op("dve", lambda e: e.tensor_copy(out=ost[hp:hp + 64, op_, :], in_=psum[hp:hp + 64, ob, :]),
                     reads=[R_bank[ob]], writes=[R_ost[op_]])
                P.dma("sp", bfv(ot_c[hl // 4])[(hl % 4) * 64:(hl % 4) * 64 + 64, g * TT:(g + 1) * TT], ost[hp:hp + 64, op_, :], f"ost{op_}",
                      reads=[R_ost[op_]], writes=[R_otc[hl // 4]])

        load_head(0)
        load_v()
        st_z(0)
        for t in range(N + 3):
            if t + 1 < N:
                st_z(t + 1)
            if t < N:
                st_prep(t)
                st_e(t)
            if 0 <= t - 2 < N:
                st_x(t - 2)
            if t < N:
                st_sp(t)
            if 0 <= t - 1 < N:
                st_c(t - 1)
            if 0 <= t - 3 < N:
                st_v(t - 3)
        return mine

    prev = attention()
    for i in range(2):
        P.cc(ot_c[i], ot_g[i], reads=[R_otc[i]], writes=[R_otg[i]])
    o = O_PH
    ows = [view(o + s * 8192, [KC, 256], F32) for s in range(2)]; o += 16384
    owb = [view(o + s * 4096, [KC, 256], BF16) for s in range(2)]; o += 8192
    R_ows = [Res("ows0"), Res("ows1")]
    R_owb = [Res("owb0"), Res("owb1")]
    P.barrier(R_ows + R_owb, prev)

    def load_oT():
        for c in (0, 1, 4, 5, 2, 3, 6, 7):
            def ld_o(e, c=c):
                role = get_role(e)
                r0 = (c // 4) * 256 + (c % 2) * 128
                return e.dma_start(out=hT[:, c, :], in_=bfv(ot_g[(c % 4) // 2])[r0:r0 + 128, bass.ds(role * NT, NT)])
            dma_fn("sp", ld_o, "oT", reads=[R_otg[(c % 4) // 2]], writes=[R_oT[c]])
    R_oT = [Res(f"oT{c}") for c in range(KC)]
    P.barrier(R_oT, R_hT)
    proj_accum(W["b_w_o"], hT, [R_oT] * NTT, ows, owb, R_ows, R_owb, [0], "ows", pre_hook=load_oT)
    P.barrier(R_hT, R_oT)
    tap()
    P.barrier(ALL_FFN, R_ows + R_owb + prev)
    ffn(W["f2w13_1"], W["f2w2_1"], V_F2N1)

    store_out(y_d, "out")

    with stack:
        with nc.Block() as block:
            P.emit(nc, block, stack)
    return nc


def _prep_inputs(inputs):
    f32 = np.float32
    g = {k: np.ascontiguousarray(np.asarray(v, dtype=f32)) for k, v in inputs.items()}

    def col8(v):
        return v.reshape(8, 128).T

    vec = np.zeros((128, NV), f32)
    vec[:, V_F1N0:V_F1N0 + 8] = col8(g["ffn1_norm"][0])
    vec[:, V_MIX0:V_MIX0 + 8] = col8(g["mix_norm"][0])
    vec[:, V_F2N0:V_F2N0 + 8] = col8(g["ffn2_norm"][0])
    vec[:, V_KVN:V_KVN + 8] = col8(g["kv_norm"])
    vec[:, V_F1N1:V_F1N1 + 8] = col8(g["ffn1_norm"][1])
    vec[:, V_MIX1:V_MIX1 + 8] = col8(g["mix_norm"][1])
    vec[:, V_F2N1:V_F2N1 + 8] = col8(g["ffn2_norm"][1])
    for k in range(4):
        vec[:, V_CONVW + 8 * k:V_CONVW + 8 * k + 8] = col8(g["a_conv_w"][0, k])
    vec[:, V_CONVB:V_CONVB + 8] = col8(g["a_conv_b"][0])
    vec[:, V_BR:V_BR + 8] = col8(g["a_b_r"][0])
    vec[:, V_BI:V_BI + 8] = col8(g["a_b_i"][0])
    vec[:, V_LAM:V_LAM + 8] = col8(g["a_lambda"][0])
    vec[:, V_KG] = np.tile(g["k_norm"], 2)
    vec[:, V_QG] = np.tile(g["q_norm"][0], 2)

    j = np.arange(128)[:, None]
    k = np.arange(128)[None, :]
    cst = np.zeros((128, 6, 128), f32)
    cst[:, 0] = (j == k)
    cst[:, 1] = -(j >= k).astype(f32)
    cst[:, 2] = -1.0
    cst[:, 3] = (j < k)
    cst[:, 4] = ((j // 64) == (k // 64)) / 64.0
    cst[:, 5] = 1.0 / 1024.0
    cst = cst.reshape(128, 768)
    rolec = np.zeros((2, 128, 2), f32)
    rolec[1, :, 0] = 1.0
    rolec[0, :, 1] = 1.0

    shared = {"vec": vec, "cst": cst, "rolec": rolec}
    for l in range(2):
        shared[f"f1w13_{l}"] = g["ffn1_w13"][l]
        shared[f"f1w2_{l}"] = g["ffn1_w2"][l]
        shared[f"f2w13_{l}"] = g["ffn2_w13"][l]
        shared[f"f2w2_{l}"] = g["ffn2_w2"][l]
    shared["a_w_in"] = g["a_w_in"][0]
    shared["a_w_r"] = g["a_w_r"][0]
    shared["a_w_i"] = g["a_w_i"][0]
    shared["a_w_out"] = g["a_w_out"][0]
    shared["w_kv"] = g["w_kv"]
    shared["b_w_q"] = g["b_w_q"][0]
    shared["b_w_o"] = g["b_w_o"][0]
    in_maps = []
    for c in range(8):
        b, r = c // 2, c % 2
        m = dict(shared)
        m["x"] = np.ascontiguousarray(g["x"][b, r * NT:(r + 1) * NT, :])
        in_maps.append(m)
    return in_maps


_NC_CACHE = {}


def kernel(**inputs):
    in_maps = _prep_inputs(inputs)
    if "nc" not in _NC_CACHE:
        _NC_CACHE["nc"] = build_program()
    nc = _NC_CACHE["nc"]
    res = run_bass_kernel_spmd(nc, in_maps, core_ids=list(range(8)))
    out = np.zeros((4, 4096, DM), np.float32)
    for c in range(8):
        b, r = c // 2, c % 2
        out[b, r * NT:(r + 1) * NT, :] = res.results[c]["y"]
    return out
```

```python
import numpy as np
from contextlib import ExitStack
import concourse.bass as bass
import concourse.mybir as mybir
from concourse.bass_utils import run_bass_kernel_spmd

F32 = mybir.dt.float32
BF16 = mybir.dt.bfloat16
AF = mybir.ActivationFunctionType
ALU = mybir.AluOpType

NT = 2048
TT = 512
NTT = NT // TT
DM = 1024
KC = 8
DFF = 2816
NU = 11
EPS = 1e-6
ROT = 4000
PAIRS = [[0, 1], [2, 3], [4, 5], [6, 7]]

V_F1N0, V_MIX0, V_F2N0, V_KVN, V_F1N1, V_MIX1, V_F2N1 = 0, 8, 16, 24, 32, 40, 48
V_CONVW, V_CONVB, V_BR, V_BI, V_LAM, V_KG, V_QG = 56, 88, 96, 104, 112, 120, 121
NV = 128


class Res:
    __slots__ = ("name", "w", "r")

    def __init__(self, name):
        self.name = name
        self.w = None
        self.r = []


class Ins:
    __slots__ = ("eng", "fn", "kind", "key", "deps", "need", "sig")

    def __init__(self, eng, fn, kind, key=None):
        self.eng, self.fn, self.kind, self.key = eng, fn, kind, key
        self.deps = []
        self.need = False
        self.sig = None


class Prog:
    ENGS = ("pe", "act", "dve", "pool", "sp")

    def __init__(self):
        self.streams = {k: [] for k in self.ENGS}
        self.dma_cnt = {}
        self.ncc = 0

    def _add(self, ins, reads, writes):
        deps = []
        seen = set()

        def add(d):
            if d is None or d is ins or id(d) in seen:
                return
            seen.add(id(d))
            deps.append(d)

        raw = set()
        for r in reads:
            add(r.w)
            if r.w is not None:
                raw.add(id(r.w))
        for w in writes:
            add(w.w)
            for x in w.r:
                add(x)
        final = []
        for d in deps:
            if d.eng == ins.eng and d.kind == "c" and ins.kind == "c":
                if ins.eng == "pe":
                    continue
                if id(d) not in raw:
                    continue
            final.append(d)
        for d in final:
            d.need = True
        ins.deps = final
        for r in reads:
            r.r.append(ins)
        for w in writes:
            w.w = ins
            w.r = []
        self.streams[ins.eng].append(ins)
        return ins

    def op(self, eng, fn, reads=(), writes=()):
        return self._add(Ins(eng, fn, "c"), list(reads), list(writes))

    def dma(self, eng, out, in_, key, reads=(), writes=()):
        ins = Ins(eng, lambda e: e.dma_start(out=out, in_=in_), "d", key)
        c = self.dma_cnt.get(key, 0) + 1
        self.dma_cnt[key] = c
        ins.sig = (("D", key), c * 16, c)
        return self._add(ins, list(reads), list(writes))

    def cc(self, in_t, out_t, reads=(), writes=()):
        idx = self.ncc
        self.ncc += 1
        ins = Ins("pool", lambda e: e.collective_compute(
            "AllGather", ALU.bypass, replica_groups=PAIRS,
            ins=[in_t.ap().opt()], outs=[out_t.ap().opt()]), "cc", idx)
        ins.sig = (("C", idx), 1, 1)
        return self._add(ins, list(reads), list(writes))

    def barrier(self, new, old):
        acc = []
        seen = set()
        for o in old:
            for x in ([o.w] if o.w is not None else []) + o.r:
                if id(x) not in seen:
                    seen.add(id(x))
                    acc.append(x)
        for n in new:
            n.w = None
            n.r = list(acc)

    def emit(self, nc, block, stack):
        nsig = {}
        for eng, lst in self.streams.items():
            k = 0
            for ins in lst:
                if ins.kind == "c" and ins.need:
                    k += 1
                    ins.sig = (("E", eng, (k - 1) // ROT), (k - 1) % ROT + 1, k)
            nsig[eng] = k
        sems = {}
        for eng, k in nsig.items():
            for j in range((k + ROT - 1) // ROT):
                sems[("E", eng, j)] = stack.enter_context(nc.semaphore(f"s_{eng}{j}"))
        for key in self.dma_cnt:
            sems[("D", key)] = stack.enter_context(nc.semaphore(f"d_{key}"))
        for i in range(self.ncc):
            sems[("C", i)] = stack.enter_context(nc.semaphore(f"c_{i}"))

        def run(eng, e):
            known = {}
            for ins in self.streams[eng]:
                for d in ins.deps:
                    semkey, val, gidx = d.sig
                    cls = semkey[:2] if semkey[0] == "E" else semkey
                    if known.get(cls, 0) >= gidx:
                        continue
                    known[cls] = gidx
                    e.wait_ge(sems[semkey], val)
                bi = ins.fn(e)
                if ins.kind == "c":
                    if ins.need:
                        bi.then_inc(sems[ins.sig[0]], 1)
                elif ins.kind == "d":
                    bi.then_inc(sems[ins.sig[0]], 16)
                else:
                    bi.then_inc(sems[ins.sig[0]])
            for key, c in self.dma_cnt.items():
                pass

        self._run = run
        self._sems = sems

        @block.tensor
        def _(e):
            run("pe", e)

        @block.scalar
        def _(e):
            run("act", e)

        @block.vector
        def _(e):
            run("dve", e)

        @block.gpsimd
        def _(e):
            run("pool", e)

        @block.sync
        def _(e):
            run("sp", e)
            for key, c in self.dma_cnt.items():
                e.wait_ge(sems[("D", key)], c * 16)


def build_program(debug=False):
    nc = bass.Bass("TRN2", target_bir_lowering=False)
    P = Prog()

    def dram_in(name, shape, dt=F32):
        return nc.dram_tensor(name, list(shape), dt, kind="ExternalInput")

    _rc = {}

    def get_role(e):
        if "r" not in _rc:
            _rc["r"] = e.partition_id() % 2
        return _rc["r"]

    x_d = dram_in("x", [NT, DM]).ap()
    vec_d = dram_in("vec", [128, NV]).ap()
    cst_d = dram_in("cst", [128, 6 * 128]).ap()
    rolec_d = dram_in("rolec", [2, 128, 2]).ap()
    W = {}
    for l in range(2):
        for f in (1, 2):
            W[f"f{f}w13_{l}"] = dram_in(f"f{f}w13_{l}", [DM, 2 * DFF]).ap()
            W[f"f{f}w2_{l}"] = dram_in(f"f{f}w2_{l}", [DFF, DM]).ap()
    W["a_w_in"] = dram_in("a_w_in", [DM, 2048]).ap()
    W["a_w_r"] = dram_in("a_w_r", [4, 256, 256]).ap()
    W["a_w_i"] = dram_in("a_w_i", [4, 256, 256]).ap()
    W["a_w_out"] = dram_in("a_w_out", [DM, DM]).ap()
    W["w_kv"] = dram_in("w_kv", [DM, 2048]).ap()
    W["b_w_q"] = dram_in("b_w_q", [DM, DM]).ap()
    W["b_w_o"] = dram_in("b_w_o", [DM, DM]).ap()
    y_d = nc.dram_tensor("y", [NT, DM], F32, kind="ExternalOutput").ap()

    hh_c = nc.dram_tensor("hh_c", [128, 12], F32)
    hh_g = nc.dram_tensor("hh_g", [256, 12], F32)
    st_c = nc.dram_tensor("st_c", [128, 8], F32)
    st_g = nc.dram_tensor("st_g", [256, 8], F32)
    a_sc = nc.dram_tensor("a_sc", [128, 8, NT], F32, kind=("ExternalOutput" if debug else "Internal"))
    u_sc = nc.dram_tensor("u_sc", [128, 8, NT], F32, kind=("ExternalOutput" if debug else "Internal"))
    kt_c = [nc.dram_tensor(f"kt_c{i}", [512, NT // 2], F32) for i in range(2)]
    kt_g = [nc.dram_tensor(f"kt_g{i}", [1024, NT // 2], F32) for i in range(2)]
    qt_c = [nc.dram_tensor(f"qt_c{i}", [512, NT // 2], F32) for i in range(2)]
    qt_g = [nc.dram_tensor(f"qt_g{i}", [1024, NT // 2], F32) for i in range(2)]
    v_c = [nc.dram_tensor(f"v_c{i}", [NT, 256], F32) for i in range(2)]
    v_g = [nc.dram_tensor(f"v_g{i}", [2 * NT, 256], F32) for i in range(2)]
    ot_c = [nc.dram_tensor(f"ot_c{i}", [256, NT], F32) for i in range(2)]
    ot_g = [nc.dram_tensor(f"ot_g{i}", [512, NT], F32) for i in range(2)]

    def bfv(t):
        return t.ap().bitcast(BF16)

    stack = ExitStack()
    ARENA_WORDS = 52864
    arena = stack.enter_context(nc.sbuf_tensor("arena", [128, ARENA_WORDS], F32))
    psum = stack.enter_context(nc.psum_tensor("ps", [128, 8, 512], F32))

    def view(off, shape, dt):
        n = int(np.prod(shape))
        nbytes = n * (4 if dt == F32 else 2)
        assert off % 4 == 0 and nbytes % 4 == 0
        ap = arena[:, off // 4:(off + nbytes) // 4]
        if dt != F32:
            ap = ap.bitcast(dt)
        if len(shape) == 2:
            ap = ap.rearrange("p (a b) -> p a b", a=shape[0])
        elif len(shape) == 3:
            ap = ap.rearrange("p (a b c) -> p a b c", a=shape[0], b=shape[1])
        return ap

    O_XT = 0
    O_CST = 65536
    O_HT = O_CST + 4096
    O_PH = O_HT + 32768
    xT = view(O_XT, [KC, NT], F32)
    cbf = view(O_CST, [6, 128], BF16)
    identf = view(O_CST + 1536, [128], F32)
    vec = view(O_CST + 2048, [NV], F32)
    flag = view(O_CST + 2560, [2], F32)
    misc = view(O_CST + 2568, [64], F32)
    hT = view(O_HT, [KC, NT], BF16)
    IDB, NTRI, NONES, CMASK, BD64, ONES1K = range(6)

    R_xT = [[Res(f"xT{c}_{t}") for t in range(NTT)] for c in range(KC)]
    R_hT = [Res(f"hT{t}") for t in range(NTT)]
    R_cst = Res("cst")
    R_bank = [Res(f"bank{i}") for i in range(8)]

    O_WS13 = O_PH
    O_WS2 = O_WS13 + 2 * 16384
    O_WB13 = O_WS2 + 2 * 8192
    O_WB2 = O_WB13 + 2 * 8192
    O_REST = O_WB2 + 2 * 4096
    ws13 = [view(O_WS13 + s * 16384, [KC, 2, 256], F32) for s in range(2)]
    ws2 = [view(O_WS2 + s * 8192, [2, 1024], F32) for s in range(2)]
    wb13 = [view(O_WB13 + s * 8192, [KC, 2, 256], BF16) for s in range(2)]
    wb2 = [view(O_WB2 + s * 4096, [2, 1024], BF16) for s in range(2)]
    R_ws13p = [[Res(f"ws13_{s}_{k}") for k in range(4)] for s in range(2)]
    R_ws13 = [Res(f"ws13_{s}") for s in range(2)]
    R_ws13a = [[R_ws13[s]] + R_ws13p[s] for s in range(2)]
    R_ws2 = [Res(f"ws2_{s}") for s in range(2)]
    R_wb13 = [[Res(f"wb13_{s}_{k}") for k in range(4)] for s in range(2)]
    R_wb2 = [Res(f"wb2_{s}") for s in range(2)]
    Abuf = [view(O_REST + s * 8192, [2, NT], BF16) for s in range(2)]
    R_A = [[Res(f"A{s}_{t}") for t in range(NTT)] for s in range(2)]
    sq = view(O_REST + 16384, [KC, TT], BF16)
    R_sq = Res("sq")
    rstd = view(O_REST + 24576, [TT], F32)
    R_rstd = Res("rstd")
    sg = [view(O_REST + 26624 + s * 1024, [TT], BF16) for s in range(2)]
    R_sg = [Res(f"sg{s}") for s in range(2)]
    assert O_REST + 28672 <= ARENA_WORDS * 4

    wunit = [0]

    cstage = view(O_WS13, [6, 128], F32)
    R_cstage = Res("cstage")
    P.dma("sp", cstage, cst_d.rearrange("p (a b) -> p a b", a=6), "misc0", writes=[R_cstage])
    P.dma("sp", vec, vec_d, "misc1", writes=[R_cst])
    def _role_dma(e):
        role = get_role(e)
        return e.dma_start(out=flag, in_=rolec_d[bass.ds(role, 1), :, :].rearrange("a p b -> p (a b)"))
    ins = Ins("sp", _role_dma, "d", "misc2")
    c = P.dma_cnt.get("misc2", 0) + 1
    P.dma_cnt["misc2"] = c
    ins.sig = (("D", "misc2"), c * 16, c)
    P._add(ins, [], [R_cst])
    P.op("dve", lambda e: e.tensor_copy(out=cbf, in_=cstage), reads=[R_cstage], writes=[R_cst])
    P.op("dve", lambda e: e.tensor_copy(out=identf, in_=cstage[:, 0, :]), reads=[R_cstage], writes=[R_cst])
    P.barrier(R_ws13 + R_ws13p[0] + R_ws13p[1], [R_cstage])

    xst = [view(O_WS13 + s * 16384, [4, DM], F32) for s in range(2)]
    R_xst = R_ws13a
    x_v = x_d.rearrange("(t j p) n -> t p j n", j=4, p=128)
    R_xj = [[Res(f"xst{s}_{j}") for j in range(4)] for s in range(2)]
    P.barrier(R_xj[0] + R_xj[1], R_ws13a[0] + R_ws13a[1])
    for tt in range(NTT):
        s = tt % 2
        for j in range(4):
            P.dma("sp", xst[s][:, j, :], x_v[tt][:, j, :], f"xst{s}_{j}", writes=[R_xj[s][j]])
        for c in range(KC):
            b = c % 4
            for j in range(4):
                P.op("pe", lambda e, b=b, j=j, s=s, c=c: e.transpose(
                    psum[:, b, j * 128:(j + 1) * 128], xst[s][:, j, c * 128:(c + 1) * 128], identf),
                    reads=[R_xj[s][j], R_cst], writes=[R_bank[b]])
            eng = "act" if c % 2 == 0 else "dve"
            if eng == "act":
                P.op("act", lambda e, b=b, c=c, tt=tt: e.copy(out=xT[:, c, tt * TT:(tt + 1) * TT], in_=psum[:, b, :]),
                     reads=[R_bank[b]], writes=[R_xT[c][tt]])
            else:
                P.op("dve", lambda e, b=b, c=c, tt=tt: e.tensor_copy(out=xT[:, c, tt * TT:(tt + 1) * TT], in_=psum[:, b, :]),
                     reads=[R_bank[b]], writes=[R_xT[c][tt]])

    P.barrier(R_ws13a[0] + R_ws13a[1], R_xj[0] + R_xj[1])

    def rmsnorm(tt, gcol):
        cols = slice(tt * TT, (tt + 1) * TT)
        P.op("act", lambda e: e.activation(out=sq, in_=xT[:, :, cols], func=AF.Square),
             reads=[R_xT[c][tt] for c in range(KC)], writes=[R_sq])
        for c in range(KC):
            P.op("pe", lambda e, c=c: e.matmul(psum[:, 7, :], cbf[:, ONES1K, :], sq[:, c, :], start=(c == 0), stop=(c == KC - 1)),
                 reads=[R_sq, R_cst], writes=[R_bank[7]])
        P.op("act", lambda e: e.activation(out=rstd, in_=psum[:, 7, :], func=AF.Ln, bias=EPS),
             reads=[R_bank[7]], writes=[R_rstd])
        P.op("act", lambda e: e.activation(out=rstd, in_=rstd, func=AF.Exp, scale=-0.5), reads=[R_rstd], writes=[R_rstd])
        for c in range(KC):
            P.op("dve", lambda e, c=c: e.scalar_tensor_tensor(out=hT[:, c, cols], in0=xT[:, c, cols], scalar=vec[:, gcol + c:gcol + c + 1],
                                                              in1=rstd, op0=ALU.mult, op1=ALU.mult),
                 reads=[R_xT[c][tt], R_rstd, R_cst], writes=[R_hT[tt]])

    def ffn(w13, w2, gcol):
        for tt in range(NTT):
            rmsnorm(tt, gcol)
        w13v = w13.rearrange("(kc p) n -> p kc n", p=128)
        pyb = [0]
        pending = []
        for u in range(NU):
            s = wunit[0] % 2
            wunit[0] += 1
            j0 = u * 256
            if u == 0:
                for ci in range(2):
                    for gu in range(2):
                        k = gu * 2 + ci
                        P.dma("sp", ws13[s][:, :, gu, ci * 128:(ci + 1) * 128], w13v[:, :, gu * DFF + j0 + ci * 128:gu * DFF + j0 + (ci + 1) * 128],
                              f"ws13_{s}_p{k}", writes=[R_ws13p[s][k]])
                        P.op("pool", lambda e, s=s, gu=gu, ci=ci: e.tensor_copy(out=wb13[s][:, :, gu, ci * 128:(ci + 1) * 128],
                                                                                 in_=ws13[s][:, :, gu, ci * 128:(ci + 1) * 128]),
                             reads=[R_ws13p[s][k]], writes=[R_wb13[s][k]])
            else:
                P.dma("sp", ws13[s][:, :, 0, :], w13v[:, :, j0:j0 + 256], f"ws13_{s}", writes=[R_ws13p[s][0], R_ws13p[s][1]])
                P.dma("sp", ws13[s][:, :, 1, :], w13v[:, :, DFF + j0:DFF + j0 + 256], f"ws13_{s}", writes=[R_ws13p[s][2], R_ws13p[s][3]])
                P.op("pool", lambda e, s=s: e.tensor_copy(out=wb13[s], in_=ws13[s]), reads=R_ws13p[s], writes=R_wb13[s])
            P.dma("sp", ws2[s], w2[j0:j0 + 256, :].rearrange("(c p) n -> p c n", p=128), f"ws2_{s}", writes=[R_ws2[s]])
            P.op("pool", lambda e, s=s: e.tensor_copy(out=wb2[s], in_=ws2[s]), reads=[R_ws2[s]], writes=[R_wb2[s]])
            a = u % 2
            prev_groups = pending
            gi = 0
            for tt in range(NTT):
                cols = slice(tt * TT, (tt + 1) * TT)
                for ci in range(2):
                    pb = (tt * 2 + ci) % 2
                    bg, bu = 2 * pb, 2 * pb + 1
                    for gu, bk in ((0, bg), (1, bu)):
                        for kc in range(KC):
                            P.op("pe", lambda e, s=s, kc=kc, gu=gu, ci=ci, bk=bk, cols=cols: e.matmul(
                                psum[:, bk, :], wb13[s][:, kc, gu, ci * 128:(ci + 1) * 128], hT[:, kc, cols],
                                start=(kc == 0), stop=(kc == KC - 1)),
                                reads=[R_wb13[s][gu * 2 + ci], R_hT[tt]], writes=[R_bank[bk]])
                    P.op("act", lambda e, pb=pb, bg=bg: e.activation(out=sg[pb], in_=psum[:, bg, :], func=AF.Silu),
                         reads=[R_bank[bg]], writes=[R_sg[pb]])
                    P.op("dve", lambda e, pb=pb, bu=bu, a=a, ci=ci, cols=cols: e.tensor_tensor(
                        out=Abuf[a][:, ci, cols], in0=sg[pb], in1=psum[:, bu, :], op=ALU.mult),
                        reads=[R_sg[pb], R_bank[bu]], writes=[R_A[a][tt]])
                    for _ in range(4):
                        if gi < len(prev_groups):
                            prev_groups[gi]()
                            gi += 1

            def mk_group(s, a, tt, dc):
                def emit_group():
                    cols = slice(tt * TT, (tt + 1) * TT)
                    bk = 4 + pyb[0] % 4
                    pyb[0] += 1
                    for ci in range(2):
                        P.op("pe", lambda e, ci=ci, bk=bk: e.matmul(
                            psum[:, bk, :], wb2[s][:, ci, dc * 128:(dc + 1) * 128], Abuf[a][:, ci, cols],
                            start=(ci == 0), stop=(ci == 1)),
                            reads=[R_wb2[s], R_A[a][tt]], writes=[R_bank[bk]])
                    P.op("dve", lambda e, bk=bk: e.scalar_tensor_tensor(
                        out=xT[:, dc, cols], in0=psum[:, bk, :], scalar=0.5, in1=xT[:, dc, cols], op0=ALU.mult, op1=ALU.add),
                        reads=[R_bank[bk], R_xT[dc][tt]], writes=[R_xT[dc][tt]])
                return emit_group
            pending = [mk_group(s, a, tt, dc) for tt in range(NTT) for dc in range(KC)]
        for g_ in pending:
            g_()

    def dma_fn(eng, fn, key, reads=(), writes=()):
        ins = Ins(eng, fn, "d", key)
        c = P.dma_cnt.get(key, 0) + 1
        P.dma_cnt[key] = c
        ins.sig = (("D", key), c * 16, c)
        return P._add(ins, list(reads), list(writes))

    ALL_FFN = R_ws13 + R_ws13p[0] + R_ws13p[1] + R_ws2 + R_wb13[0] + R_wb13[1] + R_wb2 + R_A[0] + R_A[1] + [R_sq, R_rstd] + R_sg

    def store_out(y_ap, key):
        ost = [view(O_WS13 + s * 16384, [4, DM], F32) for s in range(2)]
        y_v = y_ap.rearrange("(t j p) n -> t p j n", j=4, p=128)
        R_oj = [[Res(f"ostj{s}_{j}") for j in range(4)] for s in range(2)]
        P.barrier(R_oj[0] + R_oj[1], R_ws13a[0] + R_ws13a[1])
        k = 0
        for tt in range(NTT):
            s = tt % 2
            for j in range(4):
                for half in range(2):
                    b = k % 4
                    k += 1
                    for q in range(4):
                        c = half * 4 + q
                        P.op("pe", lambda e, b=b, q=q, c=c, tt=tt, j=j: e.transpose(
                            psum[:, b, q * 128:(q + 1) * 128], xT[:, c, tt * TT + j * 128: tt * TT + (j + 1) * 128], identf),
                            reads=[R_xT[c][tt], R_cst], writes=[R_bank[b]])
                    if half == 0:
                        P.op("act", lambda e, b=b, s=s, j=j: e.copy(out=ost[s][:, j, 0:512], in_=psum[:, b, :]),
                             reads=[R_bank[b]], writes=[R_oj[s][j]])
                    else:
                        P.op("dve", lambda e, b=b, s=s, j=j: e.tensor_copy(out=ost[s][:, j, 512:1024], in_=psum[:, b, :]),
                             reads=[R_bank[b]], writes=[R_oj[s][j]])
                P.dma("sp", y_v[tt][:, j, :], ost[s][:, j, :], f"{key}{s}_{j}", reads=[R_oj[s][j]])
        P.barrier(R_ws13a[0] + R_ws13a[1], R_oj[0] + R_oj[1])

    dbg_n = [0]

    def tap():
        if debug:
            dbg_n[0] += 1
            d = nc.dram_tensor(f"dbg{dbg_n[0]}", [NT, DM], F32, kind="ExternalOutput").ap()
            store_out(d, "out")

    P.op("act", lambda e: e.activation(out=misc[:, 0:8], in_=vec[:, V_LAM:V_LAM + 8], func=AF.Exp, scale=-1.0),
         reads=[R_cst], writes=[R_cst])
    P.op("act", lambda e: e.activation(out=misc[:, 0:8], in_=misc[:, 0:8], func=AF.Ln, bias=1.0),
         reads=[R_cst], writes=[R_cst])
    P.op("dve", lambda e: e.tensor_scalar(out=misc[:, 0:8], in0=misc[:, 0:8], scalar1=-8.0, scalar2=None, op0=ALU.mult),
         reads=[R_cst], writes=[R_cst])
    P.op("dve", lambda e: e.tensor_scalar(out=misc[:, 8:9], in0=vec[:, V_QG:V_QG + 1], scalar1=0.125, scalar2=None, op0=ALU.mult),
         reads=[R_cst], writes=[R_cst])

    ffn(W["f1w13_0"], W["f1w2_0"], V_F1N0)
    tap()

    def rglru():
        for tt in range(NTT):
            rmsnorm(tt, V_MIX0)
        o = O_PH
        rws = view(o, [KC, 256], F32); o += 8192
        rwb = view(o, [KC, 256], BF16); o += 4096
        gws = view(o, [2, 2, 256], F32); o += 4096
        gwb = view(o, [2, 2, 256], BF16); o += 2048
        rec = view(o, [2, 2052], F32); o += 16416
        xc = view(o, [2, NT], F32); o += 16384
        xcb = view(o, [2, NT], BF16); o += 8192
        rbs = [view(o + q_ * 8192, [NT], F32) for q_ in range(2)]; o += 16384
        ibs = [view(o + q_ * 8192, [NT], F32) for q_ in range(2)]; o += 16384
        tbs = [view(o + q_ * 8192, [NT], F32) for q_ in range(2)]; o += 16384
        osm = O_CST + 3072
        hraw = view(osm, [24], BF16); osm += 64
        hhal = view(osm, [KC, 16], BF16); osm += 256
        stc = view(osm, [8], F32); osm += 32
        sraw = view(osm, [8], F32); osm += 32
        init = view(osm, [8], F32); osm += 32
        assert osm <= O_CST + 4096
        assert o <= ARENA_WORDS * 4
        R_rws, R_rwb, R_gws, R_gwb = Res("rws"), Res("rwb"), Res("gws"), Res("gwb")
        R_rec = [Res("rec0"), Res("rec1")]
        R_xc = [Res("xc0"), Res("xc1")]
        R_xcb = [Res("xcb0"), Res("xcb1")]
        R_rs = [Res("r0"), Res("r1")]
        R_is = [Res("i0"), Res("i1")]
        R_ts = [Res("t0"), Res("t1")]
        R_small = Res("rg_small")
        R_hhc, R_hhg, R_stcD, R_stg = Res("hh_c"), Res("hh_g"), Res("st_c"), Res("st_g")
        R_asc = [Res(f"asc{c}") for c in range(KC)]
        p1 = [R_rws, R_rwb, R_gws, R_gwb] + R_rec + R_xc + R_xcb + R_rs + R_is + R_ts + [R_small]
        P.barrier(p1, ALL_FFN)
        w_in_v = W["a_w_in"].rearrange("(kc p) n -> p kc n", p=128)

        def load_rw(n):
            P.dma("sp", rws, w_in_v[:, :, 1024 + n * 256:1024 + (n + 1) * 256], "rws", writes=[R_rws])
            P.op("pool", lambda e: e.tensor_copy(out=rwb, in_=rws), reads=[R_rws], writes=[R_rwb])
        load_rw(0)
        P.dma("sp", bfv(hh_c).rearrange("p (c k) -> p c k", k=3), hT[:, :, NT - 3:NT], "rgx", reads=[R_hT[NTT - 1]], writes=[R_hhc])
        P.cc(hh_c, hh_g, reads=[R_hhc], writes=[R_hhg])
        P.dma("sp", hraw, bfv(hh_g)[0:128, :], "rgx", reads=[R_hhg], writes=[R_small])
        P.op("dve", lambda e: e.memset(hhal, 0.0), writes=[R_small])
        P.op("dve", lambda e: e.tensor_scalar(out=hhal[:, :, 0:3], in0=hraw.rearrange("p (c k) -> p c k", k=3), scalar1=flag[:, 0:1], scalar2=None, op0=ALU.mult),
             reads=[R_small, R_cst], writes=[R_small])
        bk = [0]

        def stage_a1(n):
            if n > 0:
                load_rw(n)
            for j in range(2):
                for tt in range(NTT):
                    b = bk[0] % 4
                    bk[0] += 1
                    for kc in range(KC):
                        P.op("pe", lambda e, kc=kc, j=j, b=b, tt=tt: e.matmul(psum[:, b, :], rwb[:, kc, j * 128:(j + 1) * 128],
                                                                            hT[:, kc, tt * TT:(tt + 1) * TT], start=(kc == 0), stop=(kc == KC - 1)),
                             reads=[R_rwb, R_hT[tt]], writes=[R_bank[b]])
                    P.op("act", lambda e, j=j, b=b, tt=tt: e.copy(out=rec[:, j, 3 + tt * TT:3 + (tt + 1) * TT], in_=psum[:, b, :]),
                         reads=[R_bank[b]], writes=[R_rec[j]])
            for j in range(2):
                for kc in range(KC):
                    P.op("pe", lambda e, kc=kc, j=j: e.matmul(psum[:, 7, 0:16], rwb[:, kc, j * 128:(j + 1) * 128], hhal[:, kc, :],
                                                            start=(kc == 0), stop=(kc == KC - 1)),
                         reads=[R_rwb, R_small], writes=[R_bank[7]])
                P.op("act", lambda e, j=j: e.copy(out=rec[:, j, 0:3], in_=psum[:, 7, 0:3]), reads=[R_bank[7]], writes=[R_rec[j]])

        def stage_a2(n):
            for j in range(2):
                ch = 2 * n + j
                P.op("dve", lambda e, j=j, ch=ch: e.tensor_scalar(out=xc[:, j, :], in0=rec[:, j, 3:3 + NT],
                                                                scalar1=vec[:, V_CONVW + 24 + ch:V_CONVW + 25 + ch],
                                                                scalar2=vec[:, V_CONVB + ch:V_CONVB + ch + 1], op0=ALU.mult, op1=ALU.add),
                     reads=[R_rec[j], R_cst], writes=[R_xc[j]])
                for k in range(3):
                    P.op("dve", lambda e, j=j, ch=ch, k=k: e.scalar_tensor_tensor(
                        out=xc[:, j, :], in0=rec[:, j, k:k + NT], scalar=vec[:, V_CONVW + 8 * k + ch:V_CONVW + 8 * k + ch + 1],
                        in1=xc[:, j, :], op0=ALU.mult, op1=ALU.add),
                        reads=[R_rec[j], R_xc[j], R_cst], writes=[R_xc[j]])
                P.op("act", lambda e, j=j: e.copy(out=xcb[:, j, :], in_=xc[:, j, :]), reads=[R_xc[j]], writes=[R_xcb[j]])

        def stage_b(n):
            P.dma("sp", gws[:, 0], W["a_w_r"][n].rearrange("(cc p) d -> p cc d", p=128), "gws", writes=[R_gws])
            P.dma("sp", gws[:, 1], W["a_w_i"][n].rearrange("(cc p) d -> p cc d", p=128), "gws", writes=[R_gws])
            P.op("pool", lambda e: e.tensor_copy(out=gwb, in_=gws), reads=[R_gws], writes=[R_gwb])
            for j in range(2):
                ch = 2 * n + j
                for gi, (buf, Rb, bcol) in enumerate(((rbs[j], R_rs[j], V_BR), (ibs[j], R_is[j], V_BI))):
                    for tt in range(NTT):
                        b = bk[0] % 4
                        bk[0] += 1
                        for cc in range(2):
                            P.op("pe", lambda e, gi=gi, cc=cc, j=j, b=b, tt=tt: e.matmul(
                                psum[:, b, :], gwb[:, gi, cc, j * 128:(j + 1) * 128], xcb[:, cc, tt * TT:(tt + 1) * TT],
                                start=(cc == 0), stop=(cc == 1)),
                                reads=[R_gwb, R_xcb[cc]], writes=[R_bank[b]])
                        P.op("act", lambda e, buf=buf, b=b, tt=tt, bcol=bcol, ch=ch: e.activation(
                            out=buf[:, tt * TT:(tt + 1) * TT], in_=psum[:, b, :], func=AF.Sigmoid, bias=vec[:, bcol + ch:bcol + ch + 1]),
                            reads=[R_bank[b], R_cst], writes=[Rb])
            for j in range(2):
                ch = 2 * n + j
                P.op("act", lambda e, ch=ch, rb=rbs[j]: e.activation(out=rb, in_=rb, func=AF.Exp, scale=misc[:, ch:ch + 1]),
                     reads=[R_rs[j], R_cst], writes=[R_rs[j]])
            for j in range(2):
                P.op("dve", lambda e, rb=rbs[j], tb_=tbs[j]: e.scalar_tensor_tensor(out=tb_, in0=rb, scalar=-1.0, in1=rb, op0=ALU.mult, op1=ALU.mult),
                     reads=[R_rs[j]], writes=[R_ts[j]])
            for j in range(2):
                P.op("act", lambda e, tb_=tbs[j]: e.activation(out=tb_, in_=tb_, func=AF.Sqrt, bias=1.0), reads=[R_ts[j]], writes=[R_ts[j]])
            for j in range(2):
                ch = 2 * n + j
                rb, ib, tb_ = rbs[j], ibs[j], tbs[j]
                R_r, R_i, R_t = R_rs[j], R_is[j], R_ts[j]
                P.op("dve", lambda e, ib=ib, tb_=tb_: e.tensor_tensor(out=ib, in0=ib, in1=tb_, op=ALU.mult), reads=[R_i, R_t], writes=[R_i])
                P.op("dve", lambda e, j=j, ib=ib: e.tensor_tensor(out=ib, in0=ib, in1=xc[:, j, :], op=ALU.mult), reads=[R_i, R_xc[j]], writes=[R_i])
                P.dma("sp", a_sc[:, ch, :], rb, f"rgsp{ch}", reads=[R_r], writes=[R_asc[ch]])
                P.dma("sp", u_sc[:, ch, :], ib, f"rgsp{ch}", reads=[R_i], writes=[R_asc[ch]])
                P.op("dve", lambda e, rb=rb, ib=ib, tb_=tb_: e.tensor_tensor_scan(out=tb_, data0=rb, data1=ib, initial=0.0, op0=ALU.mult, op1=ALU.add),
                     reads=[R_r, R_i, R_t], writes=[R_t])
                P.op("dve", lambda e, ch=ch, tb_=tb_: e.tensor_copy(out=stc[:, ch:ch + 1], in_=tb_[:, NT - 1:NT]), reads=[R_t], writes=[R_small])

        stage_a1(0)
        stage_a2(0)
        for n in range(4):
            if n + 1 < 4:
                stage_a1(n + 1)
            stage_b(n)
            if n + 1 < 4:
                stage_a2(n + 1)
        P.dma("sp", st_c.ap(), stc, "rgx", reads=[R_small], writes=[R_stcD])
        P.cc(st_c, st_g, reads=[R_stcD], writes=[R_stg])
        def load_init():
            P.dma("sp", sraw, st_g[0:128, :], "rgx", reads=[R_stg], writes=[R_small])
            P.op("dve", lambda e: e.tensor_scalar(out=init, in0=sraw, scalar1=flag[:, 0:1], scalar2=None, op0=ALU.mult),
                 reads=[R_small, R_cst], writes=[R_small])
        if debug:
            load_init()
            for nm, ap_, n in (("d_init", init, 8), ("d_flag", flag, 2), ("d_sraw", sraw, 8), ("d_stc", stc, 8)):
                dd = nc.dram_tensor(nm, [128, n], F32, kind="ExternalOutput").ap()
                P.dma("sp", dd, ap_, "out", reads=[R_small, R_cst])
        o = O_PH
        pws = [view(o + s * 8192, [KC, 256], F32) for s in range(2)]; o += 16384
        pwb = [view(o + s * 4096, [KC, 256], BF16) for s in range(2)]; o += 8192
        al = [view(o + s * 8192, [NT], F32) for s in range(2)]; o += 16384
        ul = [view(o + s * 8192, [NT], F32) for s in range(2)]; o += 16384
        hs = view(o, [NT], F32); o += 8192
        yT = view(o, [KC, NT], BF16); o += 32768
        gbt = [view(o + s * 2048, [TT], F32) for s in range(2)]; o += 4096
        assert o <= ARENA_WORDS * 4 - 256
        R_pws = [Res("pws0"), Res("pws1")]
        R_pwb = [Res("pwb0"), Res("pwb1")]
        R_al = [Res("al0"), Res("al1")]
        R_hs = Res("hs")
        R_yT = [Res(f"yT{t}") for t in range(NTT)]
        R_gbt = [Res("gbt0"), Res("gbt1")]
        p2 = R_pws + R_pwb + R_al + [R_hs] + R_yT + R_gbt
        P.barrier(p2, [x for x in p1 if x is not R_small])
        wu = [0]
        gb = [0]
        for n in range(4):
            s = wu[0] % 2
            wu[0] += 1
            P.dma("sp", pws[s], w_in_v[:, :, n * 256:(n + 1) * 256], f"pws{s}", writes=[R_pws[s]])
            P.op("pool", lambda e, s=s: e.tensor_copy(out=pwb[s], in_=pws[s]), reads=[R_pws[s]], writes=[R_pwb[s]])
            for j in range(2):
                ch = 2 * n + j
                pr = ch % 2
                P.dma("sp", al[pr], a_sc[:, ch, :], f"al{pr}", reads=[R_asc[ch]], writes=[R_al[pr]])
                P.dma("sp", ul[pr], u_sc[:, ch, :], f"al{pr}", reads=[R_asc[ch]], writes=[R_al[pr]])
                if ch == 0 and not debug:
                    load_init()
                P.op("dve", lambda e, pr=pr, ch=ch: e.scalar_tensor_tensor(out=ul[pr][:, 0:1], in0=al[pr][:, 0:1], scalar=init[:, ch:ch + 1],
                                                                          in1=ul[pr][:, 0:1], op0=ALU.mult, op1=ALU.add),
                     reads=[R_al[pr], R_small], writes=[R_al[pr]])
                P.op("dve", lambda e, pr=pr, ch=ch: e.tensor_tensor_scan(out=hs, data0=al[pr], data1=ul[pr], initial=0.0,
                                                                        op0=ALU.mult, op1=ALU.add),
                     reads=[R_al[pr], R_small], writes=[R_hs])
                for tt in range(NTT):
                    b = bk[0] % 4
                    bk[0] += 1
                    g2 = gb[0] % 2
                    gb[0] += 1
                    for kc in range(KC):
                        P.op("pe", lambda e, kc=kc, j=j, b=b, tt=tt, s=s: e.matmul(psum[:, b, :], pwb[s][:, kc, j * 128:(j + 1) * 128],
                                                                                 hT[:, kc, tt * TT:(tt + 1) * TT], start=(kc == 0), stop=(kc == KC - 1)),
                             reads=[R_pwb[s], R_hT[tt]], writes=[R_bank[b]])
                    P.op("act", lambda e, b=b, g2=g2: e.activation(out=gbt[g2], in_=psum[:, b, :], func=AF.Gelu_apprx_tanh),
                         reads=[R_bank[b]], writes=[R_gbt[g2]])
                    P.op("dve", lambda e, g2=g2, ch=ch, tt=tt: e.tensor_tensor(out=yT[:, ch, tt * TT:(tt + 1) * TT], in0=hs[:, tt * TT:(tt + 1) * TT],
                                                                              in1=gbt[g2], op=ALU.mult),
                         reads=[R_hs, R_gbt[g2]], writes=[R_yT[tt]])
        if debug:
            dd = nc.dram_tensor("d_yT", [128, KC, NT], BF16, kind="ExternalOutput").ap()
            P.dma("sp", dd, yT, "out", reads=R_yT)
            for nm, ap_, rr in (("d_al0", al[0], R_al[0]), ("d_al1", al[1], R_al[1]), ("d_ul0", ul[0], R_al[0]), ("d_ul1", ul[1], R_al[1]), ("d_hs", hs, R_hs)):
                dd = nc.dram_tensor(nm, [128, NT], F32, kind="ExternalOutput").ap()
                P.dma("sp", dd, ap_, "out", reads=[rr])
        proj_accum(W["a_w_out"], yT, R_yT, pws, pwb, R_pws, R_pwb, wu, "pws")
        return p2 + [R_small]

    def proj_accum(w, src, R_src, pws, pwb, R_pws, R_pwb, wu, keyp, pre_hook=None):
        wv = w.rearrange("(kc p) n -> p kc n", p=128)
        bk = 0
        for q in range(4):
            s = wu[0] % 2
            wu[0] += 1
            P.dma("sp", pws[s], wv[:, :, q * 256:(q + 1) * 256], f"{keyp}{s}", writes=[R_pws[s]])
            P.op("pool", lambda e, s=s: e.tensor_copy(out=pwb[s], in_=pws[s]), reads=[R_pws[s]], writes=[R_pwb[s]])
            if q == 0 and pre_hook is not None:
                pre_hook()
            for tt in range(NTT):
                for ci in range(2):
                    dc = 2 * q + ci
                    b = 4 + bk % 4
                    bk += 1
                    for kc in range(KC):
                        P.op("pe", lambda e, kc=kc, ci=ci, b=b, tt=tt, s=s: e.matmul(psum[:, b, :], pwb[s][:, kc, ci * 128:(ci + 1) * 128],
                                                                                  src[:, kc, tt * TT:(tt + 1) * TT], start=(kc == 0), stop=(kc == KC - 1)),
                             reads=[R_pwb[s]] + (list(R_src[tt]) if isinstance(R_src[tt], (list, tuple)) else [R_src[tt]]), writes=[R_bank[b]])
                    P.op("dve", lambda e, b=b, dc=dc, tt=tt: e.tensor_tensor(out=xT[:, dc, tt * TT:(tt + 1) * TT], in0=psum[:, b, :],
                                                                            in1=xT[:, dc, tt * TT:(tt + 1) * TT], op=ALU.add),
                         reads=[R_bank[b], R_xT[dc][tt]], writes=[R_xT[dc][tt]])

    prev = rglru()
    tap()
    P.barrier(ALL_FFN, prev)
    ffn(W["f2w13_0"], W["f2w2_0"], V_F2N0)
    tap()

    R_ktc, R_ktg, R_vc, R_vg, R_qtc, R_qtg, R_otc, R_otg = ([Res(n + "0"), Res(n + "1")] for n in ("ktc", "ktg", "vc", "vg", "qtc", "qtg", "otc", "otg"))

    def headproj(w, col0, gain_ap, dst, R_dst, gcol, with_v, mid_hook=None):
        for tt in range(NTT):
            rmsnorm(tt, gcol)
        o = O_PH
        kws = [view(o + s * 8192, [KC, 256], F32) for s in range(2)]; o += 16384
        kwb = [view(o + s * 4096, [KC, 256], BF16) for s in range(2)]; o += 8192
        sqk = [view(o + s * 1024, [TT], BF16) for s in range(3)]; o += 3072
        rsk = [view(o + s * 2048, [TT], F32) for s in range(3)]; o += 6144
        kst = [view(o + s * 4096, [NT], BF16) for s in range(2)]; o += 8192
        wvb = view(o, [KC, 1024], BF16); o += 16384
        vst = [view(o + s * 2048, [1024], BF16) for s in range(2)]; o += 4096
        vws = [view(o + s * 8192, [KC, 256], F32) for s in range(2)]; o += 16384
        R_vws = [Res("vws0"), Res("vws1")]
        assert o <= ARENA_WORDS * 4
        R_kws = [Res("kws0"), Res("kws1")]
        R_kwb = [Res("kwb0"), Res("kwb1")]
        R_sqk = [Res("sqk0"), Res("sqk1"), Res("sqk2")]
        R_rsk = [Res("rsk0"), Res("rsk1"), Res("rsk2")]
        tail = [None]
        R_kst = [Res("kst0"), Res("kst1")]
        R_wvb = Res("wvb")
        R_vst = [Res("vst0"), Res("vst1")]
        mine = R_kws + R_kwb + R_sqk + R_rsk + R_kst + [R_wvb] + R_vst + R_vws
        P.barrier(mine, ALL_FFN)
        wv = w.rearrange("(kc p) n -> p kc n", p=128)
        wu = 0
        bk = 0
        for qi, q in enumerate((0, 2, 1, 3)):
            if qi == 2 and mid_hook is not None:
                if tail[0] is not None:
                    tail[0]()
                    tail[0] = None
            s = wu % 2
            wu += 1
            P.dma("sp", kws[s], wv[:, :, col0 + q * 256:col0 + (q + 1) * 256], f"kws{s}", writes=[R_kws[s]])
            P.op("pool", lambda e, s=s: e.tensor_copy(out=kwb[s], in_=kws[s]), reads=[R_kws[s]], writes=[R_kwb[s]])
            if with_v:
                P.dma("sp", vws[qi % 2], wv[:, :, 1024 + qi * 256:1024 + (qi + 1) * 256], f"vws{qi % 2}", writes=[R_vws[qi % 2]])
                P.op("pool", lambda e, qi=qi: e.tensor_copy(out=wvb[:, :, qi * 256:(qi + 1) * 256], in_=vws[qi % 2]),
                     reads=[R_vws[qi % 2]], writes=[R_wvb])
            if qi == 3 and mid_hook is not None:
                mid_hook()
            for ci in range(2):
                oc = 2 * q + ci
                ks = oc % 2
                for tt in range(NTT):
                    b = bk % 4
                    t2 = bk % 3
                    bk += 1
                    for kc in range(KC):
                        P.op("pe", lambda e, kc=kc, ci=ci, b=b, tt=tt, s=s: e.matmul(psum[:, b, :], kwb[s][:, kc, ci * 128:(ci + 1) * 128],
                                                                                  hT[:, kc, tt * TT:(tt + 1) * TT], start=(kc == 0), stop=(kc == KC - 1)),
                             reads=[R_kwb[s], R_hT[tt]], writes=[R_bank[b]])
                    P.op("act", lambda e, b=b, t2=t2: e.activation(out=sqk[t2], in_=psum[:, b, :], func=AF.Square),
                         reads=[R_bank[b]], writes=[R_sqk[t2]])
                    if tail[0] is not None:
                        tail[0]()

                    def mk_tail(b=b, t2=t2, ks=ks, tt=tt, oc=oc, last=(tt == NTT - 1)):
                        def emit_tail():
                            mb = 4 + t2
                            P.op("pe", lambda e: e.matmul(psum[:, mb, :], cbf[:, BD64, :], sqk[t2], start=True, stop=True),
                                 reads=[R_sqk[t2], R_cst], writes=[R_bank[mb]])
                            P.op("act", lambda e: e.activation(out=rsk[t2], in_=psum[:, mb, :], func=AF.Ln, bias=EPS),
                                 reads=[R_bank[mb]], writes=[R_rsk[t2]])
                            P.op("act", lambda e: e.activation(out=rsk[t2], in_=rsk[t2], func=AF.Exp, scale=-0.5), reads=[R_rsk[t2]], writes=[R_rsk[t2]])
                            P.op("dve", lambda e: e.scalar_tensor_tensor(
                                out=kst[ks][:, tt * TT:(tt + 1) * TT], in0=psum[:, b, :], scalar=gain_ap, in1=rsk[t2], op0=ALU.mult, op1=ALU.mult),
                                reads=[R_bank[b], R_rsk[t2], R_cst], writes=[R_kst[ks]])
                            if last:
                                r0 = (oc // 4) * 256 + (oc % 2) * 128
                                P.dma("sp", bfv(dst[(oc % 4) // 2])[r0:r0 + 128, :], kst[ks], f"kst{ks}", reads=[R_kst[ks]], writes=[R_dst[(oc % 4) // 2]])
                        return emit_tail
                    tail[0] = mk_tail()
        if tail[0] is not None:
            tail[0]()
            tail[0] = None
        if with_v:
            for tb in range(16):
                vs = tb % 2
                tt = tb // 4
                for hf in range(2):
                    b = bk % 3
                    bk += 1
                    for kc in range(KC):
                        P.op("pe", lambda e, kc=kc, b=b, tb=tb, hf=hf: e.matmul(psum[:, b, :], hT[:, kc, tb * 128:(tb + 1) * 128],
                                                                              wvb[:, kc, hf * 512:(hf + 1) * 512], start=(kc == 0), stop=(kc == KC - 1)),
                             reads=[R_wvb, R_hT[tt]], writes=[R_bank[b]])
                    if hf == 0:
                        P.op("act", lambda e, b=b, vs=vs: e.copy(out=vst[vs][:, 0:512], in_=psum[:, b, :]), reads=[R_bank[b]], writes=[R_vst[vs]])
                    else:
                        P.op("dve", lambda e, b=b, vs=vs: e.tensor_copy(out=vst[vs][:, 512:1024], in_=psum[:, b, :]), reads=[R_bank[b]], writes=[R_vst[vs]])
                for i in range(2):
                    P.dma("sp", bfv(v_c[i])[tb * 128:(tb + 1) * 128, :].rearrange("p (g f) -> p g f", g=2),
                          vst[vs].rearrange("p (g i f) -> p g i f", g=2, i=2)[:, :, i, :], f"vst{vs}", reads=[R_vst[vs]], writes=[R_vc[i]])
        return mine

    prev = headproj(W["w_kv"], 0, vec[:, V_KG:V_KG + 1], kt_c, R_ktc, V_KVN, True,
                    mid_hook=lambda: P.cc(kt_c[0], kt_g[0], reads=[R_ktc[0]], writes=[R_ktg[0]]))
    P.cc(kt_c[1], kt_g[1], reads=[R_ktc[1]], writes=[R_ktg[1]])
    for i in range(2):
        P.cc(v_c[i], v_g[i], reads=[R_vc[i]], writes=[R_vg[i]])

    P.barrier(ALL_FFN, prev)
    ffn(W["f1w13_1"], W["f1w2_1"], V_F1N1)
    tap()

    prev = headproj(W["b_w_q"], 0, misc[:, 8:9], qt_c, R_qtc, V_MIX1, False,
                    mid_hook=lambda: P.cc(qt_c[0], qt_g[0], reads=[R_qtc[0]], writes=[R_qtg[0]]))
    P.cc(qt_c[1], qt_g[1], reads=[R_qtc[1]], writes=[R_qtg[1]])

    def attention():
        o = O_PH
        KT = [view(o + s * 8192, [4096], BF16) for s in range(2)]; o += 16384
        QT = [view(o + s * 8192, [4096], BF16) for s in range(2)]; o += 16384
        Vall = view(o, [32, 512], BF16); o += 32768
        ebuf = view(o, [4, TT], F32); o += 8192
        spt = view(o, [4, TT], BF16); o += 4096
        Sx = view(o, [4, TT], BF16); o += 4096
        xcb = view(o, [3, TT], F32); o += 6144
        wt = view(o, [4, TT], BF16); o += 4096
        ost = view(o, [2, TT], BF16); o += 2048
        zer = view(o, [TT], BF16); o += 1024
        assert o <= ARENA_WORDS * 4
        R_KT = [Res("KT0"), Res("KT1")]
        R_V = Res("Vall")
        R_e = [Res(f"e{i}") for i in range(4)]
        R_sp = [Res(f"sp{i}") for i in range(4)]
        R_S = [Res(f"S{i}") for i in range(4)]
        R_xc = [Res(f"xc{i}") for i in range(3)]
        R_wt = [Res(f"wt{i}") for i in range(4)]
        R_ost = [Res("ost0"), Res("ost1")]
        R_zer = Res("zer")
        mine = R_KT + [R_V] + R_e + R_sp + R_S + R_xc + R_wt + R_ost + [R_zer]
        P.barrier(mine, prev)
        P.op("pool", lambda e: e.memset(zer, 0.0), writes=[R_zer])
        for s_ in range(2):
            P.op("pool", lambda e, s_=s_: e.memset(KT[s_][64:128, :], 0.0), writes=[R_KT[s_]])
            P.op("pool", lambda e, s_=s_: e.memset(QT[s_][64:128, :], 0.0), writes=[R_KT[s_]])
        kt3 = [bfv(kt_g[i]).rearrange("(s2 rr) t -> rr s2 t", s2=2) for i in range(2)]
        qt3 = [bfv(qt_g[i]).rearrange("(s2 rr) t -> rr s2 t", s2=2) for i in range(2)]
        v3 = [bfv(v_g[i]).rearrange("(b p) f -> p b f", p=128) for i in range(2)]

        R_Vq = [Res(f"Vq{q_}") for q_ in range(4)]
        P.barrier(R_Vq, [R_V])

        def load_v():
            for q_ in range(4):
                for i in range(2):
                    def ld_v(e, i=i, q_=q_):
                        role = get_role(e)
                        return e.dma_start(out=Vall[:, q_ * 8:(q_ + 1) * 8, i * 256:(i + 1) * 256],
                                           in_=v3[i][:, q_ * 8:(q_ + 1) * 8, bass.ds(role * 256, 256)])
                    dma_fn("sp", ld_v, f"vall{q_}", reads=[R_vg[i]], writes=[R_Vq[q_]])

        tiles = []
        ngrp = 0
        for hl in range(8):
            for g in range(8):
                ob = 5 + ngrp % 2
                ngrp += 1
                kbs = list(range(4 * g + 3, -1, -1))
                for n_, kb in enumerate(kbs):
                    tiles.append(dict(hl=hl, g=g, kb=kb, c0=max(0, kb - 4 * g) * 128, first=(n_ == 0), last=(n_ == len(kbs) - 1),
                                      ob=ob, diag=(kb >= 4 * g), gi=ngrp - 1))
        N = len(tiles)
        loaded = set()

        def load_head(hl):
            if hl in loaded or hl >= 8:
                return
            loaded.add(hl)
            s = hl % 2

            def ld_k(e, s=s, hl=hl):
                role = get_role(e)
                return e.dma_start(out=KT[s][0:64, :].rearrange("p (a t) -> p a t", a=2),
                                   in_=kt3[hl // 4][bass.ds((role * 256 + (hl % 4) * 64) if hl % 4 else role * 256, 64), :, :])

            def ld_q(e, s=s, hl=hl):
                role = get_role(e)
                return e.dma_start(out=QT[s][0:64, :].rearrange("p (a t) -> p a t", a=2),
                                   in_=qt3[hl // 4][bass.ds((role * 256 + (hl % 4) * 64) if hl % 4 else role * 256, 64), :, :])
            dma_fn("sp", ld_k, f"kq{s}", reads=[R_ktg[hl // 4]], writes=[R_KT[s]])
            dma_fn("sp", ld_q, f"kq{s}", reads=[R_qtg[hl // 4]], writes=[R_KT[s]])

        def st_z(t):
            T = tiles[t]
            load_head(T["hl"])
            load_head(T["hl"] + 1)
            s, kb, c0, g, z = T["hl"] % 2, T["kb"], T["c0"], T["g"], t % 2
            P.op("pe", lambda e: e.matmul(psum[:, z, c0:TT], KT[s][:, kb * 128:(kb + 1) * 128], QT[s][:, g * TT + c0:(g + 1) * TT],
                                          start=True, stop=True), reads=[R_KT[s]], writes=[R_bank[z]])
            P.op("pe", lambda e: e.matmul(psum[:, 7, :], cbf[:, NONES, :], zer, start=True, stop=True),
                 reads=[R_zer, R_cst], writes=[R_bank[7]])

        def st_prep(t):
            T = tiles[t]
            r, c0 = t % 4, T["c0"]
            if T["first"]:
                P.op("dve", lambda e: e.memset(Sx[:, r, :], 0.0), writes=[R_S[r]])

        def st_e(t):
            T = tiles[t]
            r, c0, z = t % 4, T["c0"], t % 2
            P.op("act", lambda e: e.activation(out=ebuf[:, r, c0:TT], in_=psum[:, z, c0:TT], func=AF.Exp),
                 reads=[R_bank[z]], writes=[R_e[r]])

        def st_sp(t):
            T = tiles[t]
            r, c0 = t % 4, T["c0"]
            P.op("act", lambda e: e.activation(out=spt[:, r, c0:TT], in_=ebuf[:, r, c0:TT], func=AF.Ln, bias=1.0),
                 reads=[R_e[r]], writes=[R_sp[r]])
            if T["diag"]:
                P.op("dve", lambda e: e.tensor_tensor(out=spt[:, r, c0:c0 + 128], in0=spt[:, r, c0:c0 + 128], in1=cbf[:, CMASK, :], op=ALU.mult),
                     reads=[R_sp[r], R_cst], writes=[R_sp[r]])
            rn = (t + 1) % 4
            if c0 > 0:
                P.op("dve", lambda e: e.memset(Sx[:, rn, 0:c0], 0.0), writes=[R_S[rn]])
            P.op("dve", lambda e: e.tensor_tensor(out=Sx[:, rn, c0:TT], in0=Sx[:, r, c0:TT], in1=spt[:, r, c0:TT], op=ALU.add),
                 reads=[R_S[r], R_sp[r]], writes=[R_S[rn]])

        def st_c(t):
            T = tiles[t]
            r, c0, l, first = t % 4, T["c0"], 2 + t % 3, T["first"]
            P.op("pe", lambda e: e.matmul(psum[:, l, c0:TT], cbf[:, NTRI, :], spt[:, r, c0:TT], start=True, stop=first),
                 reads=[R_sp[r], R_cst], writes=[R_bank[l]])
            if not first:
                P.op("pe", lambda e: e.matmul(psum[:, l, c0:TT], cbf[:, NONES, :], Sx[:, r, c0:TT], start=False, stop=True),
                     reads=[R_S[r], R_cst], writes=[R_bank[l]])

        def st_x(t):
            T = tiles[t]
            r, c0, l, x_ = t % 4, T["c0"], 2 + t % 3, t % 3
            P.op("act", lambda e: e.activation(out=xcb[:, x_, c0:TT], in_=psum[:, l, c0:TT], func=AF.Exp),
                 reads=[R_bank[l]], writes=[R_xc[x_]])
            P.op("dve", lambda e: e.tensor_tensor(out=wt[:, r, c0:TT], in0=xcb[:, x_, c0:TT], in1=ebuf[:, r, c0:TT], op=ALU.mult),
                 reads=[R_xc[x_], R_e[r]], writes=[R_wt[r]])
            if T["diag"]:
                P.op("dve", lambda e: e.tensor_tensor(out=wt[:, r, c0:c0 + 128], in0=wt[:, r, c0:c0 + 128], in1=cbf[:, CMASK, :], op=ALU.mult),
                     reads=[R_wt[r], R_cst], writes=[R_wt[r]])

        def st_v(t):
            T = tiles[t]
            r, c0, ob, hl, kb, g = t % 4, T["c0"], T["ob"], T["hl"], T["kb"], T["g"]
            vsl = slice((hl // 2) * 128, (hl // 2 + 1) * 128)
            if T["first"]:
                P.op("pe", lambda e: e.matmul(psum[:, ob, :], Vall[:, 0, vsl], zer, start=True, stop=False),
                     reads=[R_Vq[0], R_zer], writes=[R_bank[ob]])
            last = T["last"]
            P.op("pe", lambda e: e.matmul(psum[:, ob, c0:TT], Vall[:, kb, vsl], wt[:, r, c0:TT], start=False, stop=last),
                 reads=[R_Vq[kb // 8], R_wt[r]], writes=[R_bank[ob]])
            if last:
                op_ = T["gi"] % 2
                hp = (hl % 2) * 64
                P.op("dve", lambda e: e.tensor_copy(out=ost[hp:hp + 64, op_, :], in_=psum[hp:hp + 64, ob, :]),
                     reads=[R_bank[ob]], writes=[R_ost[op_]])
                P.dma("sp", bfv(ot_c[hl // 4])[(hl % 4) * 64:(hl % 4) * 64 + 64, g * TT:(g + 1) * TT], ost[hp:hp + 64, op_, :], f"ost{op_}",
                      reads=[R_ost[op_]], writes=[R_otc[hl // 4]])

        load_head(0)
        load_v()
        st_z(0)
        for t in range(N + 3):
            if t + 1 < N:
                st_z(t + 1)
            if t < N:
                st_prep(t)
                st_e(t)
            if 0 <= t - 2 < N:
                st_x(t - 2)
            if t < N:
                st_sp(t)
            if 0 <= t - 1 < N:
                st_c(t - 1)
            if 0 <= t - 3 < N:
                st_v(t - 3)
        return mine + R_Vq

    prev = attention()
    for i in range(2):
        P.cc(ot_c[i], ot_g[i], reads=[R_otc[i]], writes=[R_otg[i]])
    o = O_PH
    ows = [view(o + s * 8192, [KC, 256], F32) for s in range(2)]; o += 16384
    owb = [view(o + s * 4096, [KC, 256], BF16) for s in range(2)]; o += 8192
    R_ows = [Res("ows0"), Res("ows1")]
    R_owb = [Res("owb0"), Res("owb1")]
    P.barrier(R_ows + R_owb, prev)

    def load_oT():
        for c in (0, 1, 4, 5, 2, 3, 6, 7):
            def ld_o(e, c=c):
                role = get_role(e)
                r0 = (c // 4) * 256 + (c % 2) * 128
                return e.dma_start(out=hT[:, c, :], in_=bfv(ot_g[(c % 4) // 2])[r0:r0 + 128, bass.ds(role * NT, NT)])
            dma_fn("sp", ld_o, "oT", reads=[R_otg[(c % 4) // 2]], writes=[R_oT[c]])
    R_oT = [Res(f"oT{c}") for c in range(KC)]
    P.barrier(R_oT, R_hT)
    proj_accum(W["b_w_o"], hT, [R_oT] * NTT, ows, owb, R_ows, R_owb, [0], "ows", pre_hook=load_oT)
    P.barrier(R_hT, R_oT)
    tap()
    P.barrier(ALL_FFN, R_ows + R_owb + prev)
    ffn(W["f2w13_1"], W["f2w2_1"], V_F2N1)

    store_out(y_d, "out")

    with stack:
        with nc.Block() as block:
            P.emit(nc, block, stack)
    return nc


def _prep_inputs(inputs):
    f32 = np.float32
    g = {k: np.ascontiguousarray(np.asarray(v, dtype=f32)) for k, v in inputs.items()}

    def col8(v):
        return v.reshape(8, 128).T

    vec = np.zeros((128, NV), f32)
    vec[:, V_F1N0:V_F1N0 + 8] = col8(g["ffn1_norm"][0])
    vec[:, V_MIX0:V_MIX0 + 8] = col8(g["mix_norm"][0])
    vec[:, V_F2N0:V_F2N0 + 8] = col8(g["ffn2_norm"][0])
    vec[:, V_KVN:V_KVN + 8] = col8(g["kv_norm"])
    vec[:, V_F1N1:V_F1N1 + 8] = col8(g["ffn1_norm"][1])
    vec[:, V_MIX1:V_MIX1 + 8] = col8(g["mix_norm"][1])
    vec[:, V_F2N1:V_F2N1 + 8] = col8(g["ffn2_norm"][1])
    for k in range(4):
        vec[:, V_CONVW + 8 * k:V_CONVW + 8 * k + 8] = col8(g["a_conv_w"][0, k])
    vec[:, V_CONVB:V_CONVB + 8] = col8(g["a_conv_b"][0])
    vec[:, V_BR:V_BR + 8] = col8(g["a_b_r"][0])
    vec[:, V_BI:V_BI + 8] = col8(g["a_b_i"][0])
    vec[:, V_LAM:V_LAM + 8] = col8(g["a_lambda"][0])
    vec[:, V_KG] = np.tile(g["k_norm"], 2)
    vec[:, V_QG] = np.tile(g["q_norm"][0], 2)

    j = np.arange(128)[:, None]
    k = np.arange(128)[None, :]
    cst = np.zeros((128, 6, 128), f32)
    cst[:, 0] = (j == k)
    cst[:, 1] = -(j >= k).astype(f32)
    cst[:, 2] = -1.0
    cst[:, 3] = (j < k)
    cst[:, 4] = ((j // 64) == (k // 64)) / 64.0
    cst[:, 5] = 1.0 / 1024.0
    cst = cst.reshape(128, 768)
    rolec = np.zeros((2, 128, 2), f32)
    rolec[1, :, 0] = 1.0
    rolec[0, :, 1] = 1.0

    shared = {"vec": vec, "cst": cst, "rolec": rolec}
    for l in range(2):
        shared[f"f1w13_{l}"] = g["ffn1_w13"][l]
        shared[f"f1w2_{l}"] = g["ffn1_w2"][l]
        shared[f"f2w13_{l}"] = g["ffn2_w13"][l]
        shared[f"f2w2_{l}"] = g["ffn2_w2"][l]
    shared["a_w_in"] = g["a_w_in"][0]
    shared["a_w_r"] = g["a_w_r"][0]
    shared["a_w_i"] = g["a_w_i"][0]
    shared["a_w_out"] = g["a_w_out"][0]
    shared["w_kv"] = g["w_kv"]
    shared["b_w_q"] = g["b_w_q"][0]
    shared["b_w_o"] = g["b_w_o"][0]
    in_maps = []
    for c in range(8):
        b, r = c // 2, c % 2
        m = dict(shared)
        m["x"] = np.ascontiguousarray(g["x"][b, r * NT:(r + 1) * NT, :])
        in_maps.append(m)
    return in_maps


_NC_CACHE = {}


def kernel(**inputs):
    in_maps = _prep_inputs(inputs)
    if "nc" not in _NC_CACHE:
        _NC_CACHE["nc"] = build_program()
    nc = _NC_CACHE["nc"]
    res = run_bass_kernel_spmd(nc, in_maps, core_ids=list(range(8)))
    out = np.zeros((4, 4096, DM), np.float32)
    for c in range(8):
        b, r = c // 2, c % 2
        out[b, r * NT:(r + 1) * NT, :] = res.results[c]["y"]
    return out
```
